# Optimizing a Trainium2 kernel written in Bass

```python
import jax, jax.numpy as jnp
from jax import lax
import numpy as np

D_MODEL = 1024
BATCH = 8
SEQ = 2048
DEPTH = 2
DEC_BATCH = 32
DEC_SEQ = 64
PAST_LEN = 1024

CHUNK = 64
N_LEFT_CHUNKS = 8
BAND_CHUNKS = N_LEFT_CHUNKS + 1
BAND_ROWS = N_LEFT_CHUNKS * CHUNK
N_A_LAYERS = DEPTH // 2
N_B_LAYERS = DEPTH - N_A_LAYERS
HEAD_DIM = 64
LRU_WIDTH = 3 * D_MODEL // 4
LRU_BLOCKS = LRU_WIDTH // HEAD_DIM
LRU_BLOCK = LRU_WIDTH // LRU_BLOCKS
CONV_WIDTH = 4
LRU_C = 8.0
ATT_WIDTH = 3 * D_MODEL // 4
N_ATT_HEADS = ATT_WIDTH // HEAD_DIM
N_MEM = 256
N_MEM_HEADS = 4
MEM_WIDTH = N_MEM_HEADS * HEAD_DIM
MIX_WIDTH = LRU_WIDTH + MEM_WIDTH
D_FF = -(-8 * D_MODEL // (3 * 256)) * 256
REL_CLIP = 256
RMS_EPS = 1e-6
NEG_INF = -1e30

kernel_name = 'yoco_rglru_chunkband_stream_step'


def rmsnorm(x, g):
    xf = x.astype(jnp.float32)
    y = xf * lax.rsqrt(jnp.mean(xf * xf, axis=-1, keepdims=True) + RMS_EPS)
    return (y * g.astype(jnp.float32)).astype(x.dtype)


def causal_conv(x, buf, w, b):
    T = x.shape[1]
    xp = jnp.concatenate([buf.astype(x.dtype), x], axis=1)
    y = b + xp[:, CONV_WIDTH - 1:CONV_WIDTH - 1 + T] * w[CONV_WIDTH - 1]
    for k in range(CONV_WIDTH - 1):
        y = y + xp[:, k:k + T] * w[k]
    return y, xp[:, T:]


def rglru(x, h0, w_a, b_a, w_i, b_i, lam, reset_first):
    B, T, _ = x.shape
    xb = x.reshape(B, T, LRU_BLOCKS, LRU_BLOCK)
    r = jax.nn.sigmoid(jnp.einsum('btni,nij->btnj', xb, w_a).reshape(B, T, LRU_WIDTH) + b_a)
    i = jax.nn.sigmoid(jnp.einsum('btni,nij->btnj', xb, w_i).reshape(B, T, LRU_WIDTH) + b_i)
    log_a = -LRU_C * r.astype(jnp.float32) * jax.nn.softplus(-lam.astype(jnp.float32))
    a = jnp.exp(log_a)
    mult = jnp.sqrt(-jnp.expm1(2.0 * log_a))
    if reset_first:
        mult = mult.at[:, 0].set(1.0)
    u = mult * (i * x).astype(jnp.float32)
    u = u.at[:, 0].add(a[:, 0] * h0.astype(jnp.float32))

    def combine(left, right):
        a_l, b_l = left
        a_r, b_r = right
        return a_l * a_r, a_r * b_l + b_r

    _, h = lax.associative_scan(combine, (a, u), axis=1)
    return h.astype(x.dtype), h[:, -1].astype(h0.dtype)


def rel_bias_lookup(rel, dist):
    idx = jnp.clip(dist, -REL_CLIP, REL_CLIP) + REL_CLIP
    return jnp.moveaxis(rel[idx], -1, 0).astype(jnp.float32)


def band_attend_prompt(q, k, v, rel):
    B, T, _ = q.shape
    NC = T // CHUNK
    BAND = BAND_CHUNKS * CHUNK
    qh = q.reshape(B, NC, CHUNK, N_ATT_HEADS, HEAD_DIM)
    pad = jnp.zeros((B, N_LEFT_CHUNKS * CHUNK, N_ATT_HEADS, HEAD_DIM), k.dtype)
    kc = jnp.concatenate([pad, k], axis=1).reshape(B, NC + N_LEFT_CHUNKS, CHUNK, N_ATT_HEADS, HEAD_DIM)
    vc = jnp.concatenate([pad.astype(v.dtype), v], axis=1).reshape(B, NC + N_LEFT_CHUNKS, CHUNK, N_ATT_HEADS, HEAD_DIM)
    kb = jnp.concatenate([kc[:, o:o + NC] for o in range(BAND_CHUNKS)], axis=2)
    vb = jnp.concatenate([vc[:, o:o + NC] for o in range(BAND_CHUNKS)], axis=2)
    s = jnp.einsum('bcqhd,bckhd->bhcqk', qh, kb).astype(jnp.float32) * (HEAD_DIM ** -0.5)
    qi = jnp.arange(CHUNK)[:, None]
    kj = jnp.arange(BAND)[None, :]
    bias = rel_bias_lookup(rel, N_LEFT_CHUNKS * CHUNK + qi - kj)
    kpos = (jnp.arange(NC)[:, None] - N_LEFT_CHUNKS) * CHUNK + jnp.arange(BAND)[None, :]
    s = s + bias[None, :, None]
    s = jnp.where((kpos >= 0)[None, None, :, None, :], s, NEG_INF)
    p = jax.nn.softmax(s, axis=-1).astype(v.dtype)
    o = jnp.einsum('bhcqk,bckhd->bcqhd', p, vb)
    return o.reshape(B, T, ATT_WIDTH)


def band_attend_sample(q, k_all, v_all, rel):
    B, T, _ = q.shape
    L = k_all.shape[1] - T
    qh = q.reshape(B, T, N_ATT_HEADS, HEAD_DIM)
    qpos = PAST_LEN + jnp.arange(T)
    kpos = PAST_LEN - L + jnp.arange(L + T)
    bias = rel_bias_lookup(rel, qpos[:, None] - kpos[None, :])
    s = jnp.einsum('bthd,bkhd->bhtk', qh, k_all).astype(jnp.float32) * (HEAD_DIM ** -0.5) + bias[None]
    p = jax.nn.softmax(s, axis=-1).astype(v_all.dtype)
    return jnp.einsum('bhtk,bkhd->bthd', p, v_all).reshape(B, T, ATT_WIDTH)


def mem_kv(mem, g, w):
    B = mem.shape[0]
    kv = rmsnorm(mem, g) @ w
    k = kv[..., :MEM_WIDTH].reshape(B, N_MEM, N_MEM_HEADS, HEAD_DIM)
    v = kv[..., MEM_WIDTH:].reshape(B, N_MEM, N_MEM_HEADS, HEAD_DIM)
    return k, v


def mem_attend(q, k, v):
    B, T, _ = q.shape
    qh = q.reshape(B, T, N_MEM_HEADS, HEAD_DIM)
    s = jnp.einsum('bthd,bmhd->bhtm', qh, k.astype(q.dtype)).astype(jnp.float32) * (HEAD_DIM ** -0.5)
    p = jax.nn.softmax(s, axis=-1).astype(q.dtype)
    return jnp.einsum('bhtm,bmhd->bthd', p, v.astype(q.dtype)).reshape(B, T, MEM_WIDTH)


def trunk(x, is_prompt, conv_buf, lru_h, past_k, past_v, mem_k, mem_v, p):
    new_conv, new_lru = [], []
    k_sh = None
    v_sh = None
    for l in range(DEPTH):
        h = rmsnorm(x, p['g_mix'][l])
        if l < N_A_LAYERS:
            proj = h @ p['w_in_a'][l]
            xr = proj[..., :LRU_WIDTH]
            gate = proj[..., LRU_WIDTH:2 * LRU_WIDTH]
            mq = proj[..., 2 * LRU_WIDTH:]
            xc, buf = causal_conv(xr, conv_buf[l], p['conv_w'][l], p['conv_b'][l])
            hs, h_last = rglru(xc, lru_h[l], p['w_rg_a'][l], p['b_rg_a'][l], p['w_rg_i'][l],
                               p['b_rg_i'][l], p['lru_lambda'][l], is_prompt)
            mix = hs * jax.nn.gelu(gate)
            new_conv.append(buf)
            new_lru.append(h_last)
        else:
            lb = l - N_A_LAYERS
            proj = h @ p['w_in_b'][lb]
            q = proj[..., :ATT_WIDTH]
            mq = proj[..., ATT_WIDTH:]
            if is_prompt:
                mix = band_attend_prompt(q, k_sh, v_sh, p['rel_bias'][lb])
            else:
                k_all = jnp.concatenate([past_k.astype(k_sh.dtype), k_sh], axis=1)
                v_all = jnp.concatenate([past_v.astype(v_sh.dtype), v_sh], axis=1)
                mix = band_attend_sample(q, k_all, v_all, p['rel_bias'][lb])
        mo = mem_attend(mq, mem_k[l], mem_v[l])
        x = x + jnp.concatenate([mix, mo], axis=-1) @ p['w_out'][l]
        hf = rmsnorm(x, p['g_ffn'][l])
        gu = hf @ p['w_ffn_gu'][l]
        x = x + (jax.nn.silu(gu[..., :D_FF]) * gu[..., D_FF:]) @ p['w_ffn_down'][l]
        if l == N_A_LAYERS - 1:
            B, T, _ = x.shape
            kv = rmsnorm(x, p['g_kv']) @ p['w_kv']
            k_sh = kv[..., :ATT_WIDTH].reshape(B, T, N_ATT_HEADS, HEAD_DIM)
            v_sh = kv[..., ATT_WIDTH:].reshape(B, T, N_ATT_HEADS, HEAD_DIM)
    y = rmsnorm(x, p['g_final'])
    return y, jnp.stack(new_conv), jnp.stack(new_lru), k_sh, v_sh


def setup_inputs(seed: int = 0) -> dict:
    key = jax.random.key(seed)
    ks = jax.random.split(key, 32)
    f32 = jnp.float32

    def nrm(k, shape, scale):
        return jax.random.normal(k, shape, f32) * scale

    band_len = min(BAND_ROWS, PAST_LEN)
    a0 = jax.random.uniform(ks[14], (N_A_LAYERS, LRU_WIDTH), f32, 0.9, 0.999)
    a = a0 ** (1.0 / LRU_C)
    lam = jnp.log(a) - jnp.log1p(-a)
    return {
        'x_prompt': nrm(ks[0], (BATCH, SEQ, D_MODEL), 1.0),
        'x_sample': nrm(ks[1], (DEC_BATCH, DEC_SEQ, D_MODEL), 1.0),
        'state_conv': nrm(ks[2], (N_A_LAYERS, DEC_BATCH, CONV_WIDTH - 1, LRU_WIDTH), 1.0),
        'state_lru': nrm(ks[3], (N_A_LAYERS, DEC_BATCH, LRU_WIDTH), 0.5),
        'cache_k': nrm(ks[4], (DEC_BATCH, band_len, N_ATT_HEADS, HEAD_DIM), 1.0),
        'cache_v': nrm(ks[5], (DEC_BATCH, band_len, N_ATT_HEADS, HEAD_DIM), 1.0),
        'cache_mem_k': nrm(ks[6], (DEPTH, DEC_BATCH, N_MEM, N_MEM_HEADS, HEAD_DIM), 1.0),
        'cache_mem_v': nrm(ks[7], (DEPTH, DEC_BATCH, N_MEM, N_MEM_HEADS, HEAD_DIM), 1.0),
        'mem_prompt': nrm(ks[8], (BATCH, N_MEM, D_MODEL), 1.0),
        'g_mix': 1.0 + nrm(ks[9], (DEPTH, D_MODEL), 0.05),
        'g_ffn': 1.0 + nrm(ks[10], (DEPTH, D_MODEL), 0.05),
        'g_final': 1.0 + nrm(ks[11], (D_MODEL,), 0.05),
        'w_in_a': nrm(ks[12], (N_A_LAYERS, D_MODEL, 2 * LRU_WIDTH + MEM_WIDTH), D_MODEL ** -0.5),
        'conv_w': nrm(ks[13], (N_A_LAYERS, CONV_WIDTH, LRU_WIDTH), CONV_WIDTH ** -0.5),
        'conv_b': nrm(ks[15], (N_A_LAYERS, LRU_WIDTH), 0.02),
        'w_rg_a': nrm(ks[16], (N_A_LAYERS, LRU_BLOCKS, LRU_BLOCK, LRU_BLOCK), LRU_BLOCK ** -0.5),
        'b_rg_a': nrm(ks[17], (N_A_LAYERS, LRU_WIDTH), 0.02),
        'w_rg_i': nrm(ks[18], (N_A_LAYERS, LRU_BLOCKS, LRU_BLOCK, LRU_BLOCK), LRU_BLOCK ** -0.5),
        'b_rg_i': nrm(ks[19], (N_A_LAYERS, LRU_WIDTH), 0.02),
        'lru_lambda': lam,
        'g_kv': 1.0 + nrm(ks[20], (D_MODEL,), 0.05),
        'w_kv': nrm(ks[21], (D_MODEL, 2 * ATT_WIDTH), D_MODEL ** -0.5),
        'w_in_b': nrm(ks[22], (N_B_LAYERS, D_MODEL, ATT_WIDTH + MEM_WIDTH), D_MODEL ** -0.5),
        'rel_bias': nrm(ks[23], (N_B_LAYERS, 2 * REL_CLIP + 1, N_ATT_HEADS), 0.5),
        'g_mem': 1.0 + nrm(ks[24], (DEPTH, D_MODEL), 0.05),
        'w_mem_kv': nrm(ks[25], (DEPTH, D_MODEL, 2 * MEM_WIDTH), D_MODEL ** -0.5),
        'w_out': nrm(ks[26], (DEPTH, MIX_WIDTH, D_MODEL), MIX_WIDTH ** -0.5),
        'w_ffn_gu': nrm(ks[27], (DEPTH, D_MODEL, 2 * D_FF), D_MODEL ** -0.5),
        'w_ffn_down': nrm(ks[28], (DEPTH, D_FF, D_MODEL), D_FF ** -0.5),
    }


def reference(x_prompt, x_sample, state_conv, state_lru, cache_k, cache_v, cache_mem_k, cache_mem_v,
              mem_prompt, g_mix, g_ffn, g_final, w_in_a, conv_w, conv_b, w_rg_a, b_rg_a, w_rg_i, b_rg_i,
              lru_lambda, g_kv, w_kv, w_in_b, rel_bias, g_mem, w_mem_kv, w_out, w_ffn_gu, w_ffn_down):
    p = {'g_mix': g_mix, 'g_ffn': g_ffn, 'g_final': g_final, 'w_in_a': w_in_a, 'conv_w': conv_w,
         'conv_b': conv_b, 'w_rg_a': w_rg_a, 'b_rg_a': b_rg_a, 'w_rg_i': w_rg_i, 'b_rg_i': b_rg_i,
         'lru_lambda': lru_lambda, 'g_kv': g_kv, 'w_kv': w_kv, 'w_in_b': w_in_b, 'rel_bias': rel_bias,
         'w_out': w_out, 'w_ffn_gu': w_ffn_gu, 'w_ffn_down': w_ffn_down}
    mk_list, mv_list = [], []
    for l in range(DEPTH):
        mk, mv = mem_kv(mem_prompt, g_mem[l], w_mem_kv[l])
        mk_list.append(mk)
        mv_list.append(mv)
    mem_k_p = jnp.stack(mk_list)
    mem_v_p = jnp.stack(mv_list)
    B = x_prompt.shape[0]
    conv0 = jnp.zeros((N_A_LAYERS, B, CONV_WIDTH - 1, LRU_WIDTH), x_prompt.dtype)
    lru0 = jnp.zeros((N_A_LAYERS, B, LRU_WIDTH), state_lru.dtype)
    y_prompt, conv_p, lru_p, k_p, v_p = trunk(x_prompt, True, conv0, lru0, None, None, mem_k_p, mem_v_p, p)
    y_sample, conv_s, lru_s, k_s, v_s = trunk(x_sample, False, state_conv, state_lru, cache_k, cache_v,
                                              cache_mem_k, cache_mem_v, p)
    nb = min(BAND_ROWS, x_prompt.shape[1])
    return (y_prompt, y_sample, conv_p, lru_p, k_p[:, -nb:], v_p[:, -nb:], mem_k_p, mem_v_p,
            conv_s, lru_s, k_s, v_s)
```

```python
import numpy as np
from contextlib import ExitStack
import concourse.bass as bass
import concourse.mybir as mybir
from concourse.bass_utils import run_bass_kernel_spmd

F32 = mybir.dt.float32
BF16 = mybir.dt.bfloat16
AF = mybir.ActivationFunctionType
ALU = mybir.AluOpType
AX = mybir.AxisListType

ENGS = ("pe", "act", "dve", "pool", "sp")
SAME_ENGINE_WINDOW = {"pe": 0, "act": 1 << 30, "dve": 1 << 30, "pool": 1 << 30, "sp": 0}
BUCKET = 1024

D = 1024
SEQ = 2048
NSB = 4
LS = 64
LRU = 768
ATT = 768
MEMW = 256
NMEM = 256
DFF = 2816
NJ = DFF // 128
EPS = 1e-6
NEG = -30000.0
NCD = {"allow_slow_non_contiguous": True}


class V:
    __slots__ = ("t", "ap", "boxes")

    def __init__(self, t, ap, boxes):
        self.t = t
        self.ap = ap
        self.boxes = boxes


def _norm_idx(shape, idx):
    if not isinstance(idx, tuple):
        idx = (idx,)
    idx = list(idx) + [slice(None)] * (len(shape) - len(idx))
    out = []
    for d, i in enumerate(idx):
        if isinstance(i, slice):
            s = 0 if i.start is None else i.start
            e = shape[d] if i.stop is None else i.stop
            assert i.step in (None, 1)
            assert 0 <= s < e <= shape[d], (shape, idx)
            out.append((s, e))
        else:
            assert 0 <= int(i) < shape[d], (shape, idx)
            out.append((int(i), int(i) + 1))
    return out


def _boxes(shape, nidx):
    p0, p1 = nidx[0]
    free = nidx[1:]
    fshape = list(shape[1:])
    strides = [int(np.prod(fshape[i + 1:])) for i in range(len(fshape))]

    def rec(d, base):
        if d == len(free):
            return [(base, base + 1)]
        if all(free[j] == (0, fshape[j]) for j in range(d + 1, len(free))):
            s, e = free[d]
            return [(base + s * strides[d], base + e * strides[d])]
        out = []
        for i in range(free[d][0], free[d][1]):
            out += rec(d + 1, base + i * strides[d])
        return out

    return [(p0, p1, a, b) for (a, b) in rec(0, 0)]


class T:
    def __init__(self, P, name, shape, dtype, space="sbuf", kind=None, track=True):
        self.name = name
        self.shape = tuple(shape)
        self.dtype = dtype
        self.space = space
        self.track = track
        nc = P.nc
        if space == "sbuf":
            self.h = P.es.enter_context(nc.sbuf_tensor(name, list(shape), dtype))
        elif space == "psum":
            self.h = P.es.enter_context(nc.psum_tensor(name, list(shape), dtype))
        else:
            self.h = nc.dram_tensor(name, list(shape), dtype, kind=kind)
        self.recs = {}

    def __getitem__(self, idx):
        nidx = _norm_idx(self.shape, idx)
        ap = self.h[idx] if self.space != "dram" else self.h.ap()[idx]
        return V(self, ap, _boxes(self.shape, nidx) if self.track else [])

    def ap(self):
        return self.h.ap()


_ESZ = {F32: 4, BF16: 2}


class Buf:
    def __init__(self, arena, off, fshape, dtype, unit):
        self.arena = arena
        self.off = off
        self.fshape = tuple(fshape)
        self.shape = (128,) + self.fshape
        self.dtype = dtype
        n = int(np.prod(fshape))
        self.scale_num = _ESZ[dtype]
        self.unit = unit
        nun = n * _ESZ[dtype] // unit
        assert n * _ESZ[dtype] % unit == 0
        self.nun = nun
        ap = arena.h[:, off:off + nun]
        if dtype != arena.dtype:
            ap = ap.bitcast(dtype)
        if len(fshape) > 1:
            names = [chr(ord("a") + i) for i in range(len(fshape))]
            kw = {nm: int(s) for nm, s in zip(names[:-1], fshape[:-1])}
            ap = ap.rearrange("p (" + " ".join(names) + ") -> p " + " ".join(names), **kw)
        self.base = ap

    def __getitem__(self, idx):
        nidx = _norm_idx(self.shape, idx)
        bx = _boxes(self.shape, nidx)
        es = _ESZ[self.dtype]
        boxes = [(p0, p1, self.off + (a * es) // self.unit, self.off + (b * es + self.unit - 1) // self.unit)
                 for (p0, p1, a, b) in bx]
        if self.arena.space == "psum":
            boxes = [(0, 128, (a // 512) * 512, ((b + 511) // 512) * 512) for (_, _, a, b) in boxes]
        return V(self.arena, self.base[idx], boxes)


class Arena:
    def __init__(self, P, name, n, dtype, space):
        self.t = T(P, name, [128, n], dtype, space)
        self.n = n
        self.top = 0
        self.unit = _ESZ[dtype]
        self.peak = 0

    def alloc(self, fshape, dtype):
        n = int(np.prod(fshape)) * _ESZ[dtype] // self.unit
        n_al = (n + 15) // 16 * 16
        off = self.top
        self.top += n_al
        self.peak = max(self.peak, self.top)
        assert self.top <= self.n, f"arena overflow {self.top} > {self.n}"
        return Buf(self.t, off, fshape, dtype, self.unit)

    def mark(self):
        return self.top

    def release(self, m):
        self.top = m


class Op:
    __slots__ = ("eng", "fn", "waits", "idx", "tok", "is_dma", "multi")


def _ovl(a, b):
    return a[0] < b[1] and b[0] < a[1] and a[2] < b[3] and b[2] < a[3]


def _cov(a, b):
    return a[0] <= b[0] and a[1] >= b[1] and a[2] <= b[2] and a[3] >= b[3]


class Prog:
    def __init__(self, nc):
        self.nc = nc
        self.es = ExitStack()
        self.ops = {e: [] for e in ENGS}
        self.count = {e: 0 for e in ENGS}
        self.waited = {e: {} for e in ENGS}
        self.needed = set()
        self.n_dma_sems = {"sp": 40, "pool": 40, "act": 8}
        self.dma_rr = {q: 0 for q in self.n_dma_sems}
        self.dma_cnt = {}
        self.final_dma = {}

    def _deps(self, reads, writes, ekey=None):
        deps = {}
        for (vs, only_w) in ((reads, True), (writes, False)):
            for v in vs:
                if not v.t.track:
                    continue
                recs = v.t.recs
                excl = v.t.space == "psum"
                for b in v.boxes:
                    for bk in range(b[2] // BUCKET, (b[3] - 1) // BUCKET + 1):
                        for (rb, key, val, w) in recs.get(bk, ()):
                            if (w or not only_w or (excl and key != ekey)) and _ovl(rb, b):
                                if deps.get(key, 0) < val:
                                    deps[key] = val
        return deps

    def _record(self, reads, writes, key, val):
        for v in writes:
            if not v.t.track:
                continue
            recs = v.t.recs
            for b in v.boxes:
                for bk in range(b[2] // BUCKET, (b[3] - 1) // BUCKET + 1):
                    lst = recs.get(bk)
                    if lst is None:
                        recs[bk] = [(b, key, val, True)]
                    else:
                        lst[:] = [r for r in lst if not _cov(b, r[0])]
                        lst.append((b, key, val, True))
        for v in reads:
            if not v.t.track:
                continue
            recs = v.t.recs
            for b in v.boxes:
                for bk in range(b[2] // BUCKET, (b[3] - 1) // BUCKET + 1):
                    lst = recs.get(bk)
                    if lst is None:
                        recs[bk] = [(b, key, val, False)]
                    else:
                        lst[:] = [r for r in lst if not (not r[3] and r[1] == key and r[0] == b)]
                        lst.append((b, key, val, False))

    def op(self, eng, fn, reads=(), writes=(), dma=False):
        reads = [r for r in reads if isinstance(r, V)]
        writes = [w for w in writes if isinstance(w, V)]
        deps = self._deps(reads, writes, ("eng", eng))
        o = Op()
        o.eng = eng
        o.fn = fn
        o.is_dma = dma
        waits = []
        if dma:
            pool = self.n_dma_sems[eng]
            si = self.dma_rr[eng]
            self.dma_rr[eng] = (si + 1) % pool
            key = ("dma", eng, si)
            prev = self.dma_cnt.get(key, 0)
            if prev:
                deps[key] = max(deps.get(key, 0), prev)
            val = prev + 16
            self.dma_cnt[key] = val
            self.final_dma[key] = val
            o.tok = (key, val)
            o.idx = None
        else:
            self.count[eng] += 1
            o.idx = self.count[eng]
            key = ("eng", eng)
            o.tok = (key, o.idx)
        wd = self.waited[eng]
        vcs = self.__dict__.setdefault("vcs", {})
        for k, v in sorted(deps.items(), key=lambda kv: -kv[1]):
            if k == ("eng", eng) and not dma:
                if (o.idx - v) > SAME_ENGINE_WINDOW[eng]:
                    continue
            if wd.get(k, 0) >= v:
                continue
            wd[k] = v
            waits.append((k, v))
            if k[0] == "eng":
                self.needed.add((k[1], v))
            pv = vcs.get((k, v))
            if pv:
                for k2, v2 in pv.items():
                    if wd.get(k2, 0) < v2:
                        wd[k2] = v2
        o.waits = waits
        snap = dict(wd)
        if not dma:
            snap.pop(("eng", eng), None)
            own = wd.get(("eng", eng), 0)
            if own:
                snap[("eng", eng)] = own
        vcs[o.tok] = snap
        self._record(reads, writes, o.tok[0], o.tok[1])
        self.ops[eng].append(o)
        return o

    def build(self, final_wait_eng="sp"):
        nc = self.nc
        es = self.es
        fo = Op()
        fo.eng = final_wait_eng
        fo.fn = None
        fo.is_dma = False
        fo.idx = None
        fo.waits = []
        for k, v in self.final_dma.items():
            if self.waited[final_wait_eng].get(k, 0) < v:
                fo.waits.append((k, v))
        for e in ENGS:
            if e != final_wait_eng and self.count[e] > 0:
                fo.waits.append((("eng", e), self.count[e]))
                self.needed.add((e, self.count[e]))
        self.ops[final_wait_eng].append(fo)
        rank = {}
        for e in ENGS:
            r = 0
            for o in self.ops[e]:
                if o.idx is not None and (e, o.idx) in self.needed:
                    r += 1
                    rank[(e, o.idx)] = r
        sems = {}
        for e in ENGS:
            sems[("eng", e)] = es.enter_context(nc.semaphore("s_" + e))
        for q, n in self.n_dma_sems.items():
            for i in range(n):
                if ("dma", q, i) in self.dma_cnt:
                    sems[("dma", q, i)] = es.enter_context(nc.semaphore(f"d_{q}_{i}"))
        block = es.enter_context(nc.Block())
        needed = self.needed

        def emit(e, h):
            for o in self.ops[e]:
                waits = [(sems[k], (rank[(k[1], v)] if k[0] == "eng" else v)) for (k, v) in o.waits]
                attach = None
                if (waits and o.fn is not None and not o.is_dma and e in ("act", "dve", "pool")
                        and not getattr(o, "multi", False)):
                    attach = waits.pop()
                for (sm, val) in waits:
                    h.wait_ge(sm, val)
                if o.fn is None:
                    continue
                ins = o.fn(h)
                if attach is not None:
                    ins._wait_ge(attach[0], attach[1])
                if o.is_dma:
                    ins.then_inc(sems[o.tok[0]], 16)
                elif (e, o.idx) in needed:
                    ins.then_inc(sems[("eng", e)], 1)

        @block.sync
        def _(h):
            emit("sp", h)

        @block.scalar
        def _(h):
            emit("act", h)

        @block.vector
        def _(h):
            emit("dve", h)

        @block.gpsimd
        def _(h):
            emit("pool", h)

        @block.tensor
        def _(h):
            emit("pe", h)

    def mm(self, out, lhsT, rhs, start=True, stop=True):
        return self.op("pe", lambda h: h.matmul(out.ap, lhsT.ap, rhs.ap, start=start, stop=stop),
                       reads=[lhsT, rhs], writes=[out])

    def transpose(self, out, in_, ident):
        return self.op("pe", lambda h: h.transpose(out.ap, in_.ap, ident.ap),
                       reads=[in_, ident], writes=[out])

    def act(self, out, in_, func, bias=None, scale=None, accum=None):
        kw = {}
        rd = [in_]
        wr = [out]
        if bias is not None:
            kw["bias"] = bias.ap if isinstance(bias, V) else bias
            rd.append(bias)
        if scale is not None:
            kw["scale"] = scale.ap if isinstance(scale, V) else scale
            rd.append(scale)
        if accum is not None:
            kw["accum_out"] = accum.ap
            wr.append(accum)
        o = self.op("act", lambda h: h.activation(out.ap, in_.ap, func, **kw), reads=rd, writes=wr)
        o.multi = accum is not None
        return o

    def tt(self, out, in0, in1, op, eng="dve"):
        return self.op(eng, lambda h: h.tensor_tensor(out.ap, in0.ap, in1.ap, op),
                       reads=[in0, in1], writes=[out])

    def ts(self, out, in0, s1, s2, op0, op1=None, eng="dve"):
        rd = [in0, s1, s2]
        a1 = s1.ap if isinstance(s1, V) else s1
        a2 = s2.ap if isinstance(s2, V) else s2
        kw = {}
        if op1 is not None:
            kw["op1"] = op1
        return self.op(eng, lambda h: h.tensor_scalar(out.ap, in0.ap, a1, a2, op0, **kw), reads=rd, writes=[out])

    def stt(self, out, in0, s, in1, op0, op1, eng="dve"):
        a = s.ap if isinstance(s, V) else s
        return self.op(eng, lambda h: h.scalar_tensor_tensor(out.ap, in0.ap, a, in1.ap, op0, op1),
                       reads=[in0, s, in1], writes=[out])

    def copy(self, out, in_, eng="dve"):
        if eng == "act":
            return self.op(eng, lambda h: h.copy(out.ap, in_.ap), reads=[in_], writes=[out])
        return self.op(eng, lambda h: h.tensor_copy(out.ap, in_.ap), reads=[in_], writes=[out])

    def memset(self, out, val, eng="pool"):
        return self.op(eng, lambda h: h.memset(out.ap, val), writes=[out])

    def rmax(self, out, in_, negate=False, eng="dve"):
        return self.op(eng, lambda h: h.tensor_reduce(out.ap, in_.ap, AX.X, ALU.max, negate=negate),
                       reads=[in_], writes=[out])

    def recip(self, out, in_, eng="dve"):
        return self.op(eng, lambda h: h.reciprocal(out.ap, in_.ap), reads=[in_], writes=[out])

    def scan(self, out, d0, d1, init):
        a = init.ap if isinstance(init, V) else init
        return self.op("dve", lambda h: h.tensor_tensor_scan(out.ap, d0.ap, d1.ap, a, ALU.mult, ALU.add),
                       reads=[d0, d1, init], writes=[out])

    def dma(self, q, out, in_, **kw):
        return self.op(q, lambda h: h.dma_start(out=out.ap, in_=in_.ap, **kw),
                       reads=[in_], writes=[out], dma=True)


PP_GMIX, PP_GFFN, PP_GFIN, PP_GKV, PP_GMEM = 0, 16, 32, 40, 48
PP_CONVW, PP_CONVB, PP_BA, PP_BI, PP_LAM = 64, 88, 94, 100, 106
PP_N = 112


class _Stop(Exception):
    pass


_DBG = {"stop": None}


def _stage(name):
    if _DBG["stop"] == name:
        raise _Stop()


def build_program():
    nc = bass.Bass("TRN2", target_bir_lowering=False)
    P = Prog(nc)

    def din(name, shape):
        return T(P, name, shape, F32, "dram", kind="ExternalInput", track=False)

    def dout(name, shape):
        return T(P, name, shape, F32, "dram", kind="ExternalOutput", track=False)

    x_p = din("x_p", [D, SEQ])
    x_s = din("x_s", [D, NSB * LS])
    st_conv = din("st_conv", [NSB, 3, LRU])
    st_lru = din("st_lru", [NSB, LRU])
    c_k = din("c_k", [NSB, 512, ATT])
    c_v = din("c_v", [NSB, 512, ATT])
    c_mk = din("c_mk", [2, NSB, NMEM, MEMW])
    c_mv = din("c_mv", [2, NSB, NMEM, MEMW])
    mem_p = din("mem_p", [D, NMEM])
    pp_d = din("pp", [128, PP_N])
    ident_d = din("ident", [128, 128])
    bias_d = din("bias_full", [128, 12, 640])
    bias_s_d = din("bias_s", [128, 12, 640])
    wa_bd_d = din("wa_bd", [128, 6, 128])
    wi_bd_d = din("wi_bd", [128, 6, 128])
    w_in_a = din("w_in_a", [D, 1792])
    w_kv = din("w_kv", [D, 1536])
    w_in_b = din("w_in_b", [D, D])
    w_mem_kv = din("w_mem_kv", [2, D, 512])
    w_out = din("w_out", [2, D, D])
    w_gu = din("w_gu", [2, D, 2 * DFF])
    w_dn = din("w_dn", [2, DFF, D])

    y_p = dout("y_p", [D, SEQ])
    y_s = dout("y_s", [D, NSB * LS])
    conv_p = dout("conv_p", [3, LRU])
    lru_p = dout("lru_p", [LRU])
    k_p = dout("k_p", [512, ATT])
    v_p = dout("v_p", [512, ATT])
    mk_p = dout("mk_p", [2, NMEM, MEMW])
    mv_p = dout("mv_p", [2, NMEM, MEMW])
    conv_s = dout("conv_s", [NSB, 3, LRU])
    lru_s = dout("lru_s", [NSB, LRU])
    k_s = dout("k_s", [NSB * LS, ATT])
    v_s = dout("v_s", [NSB * LS, ATT])

    A = Arena(P, "arena", 104000, BF16, "sbuf")
    PS = Arena(P, "psum", 4096, F32, "psum")
    banks = [PS.alloc([512], F32) for _ in range(8)]
    banks_bf = [Buf(PS.t, i * 512, [1024], BF16, 4) for i in range(8)]
    spair = [Buf(PS.t, 0, [1024], F32, 4), Buf(PS.t, 2 * 512, [1024], F32, 4)]
    st = {"bank": 0, "sp": 0}

    def nb():
        if st.get("att"):
            st["ab"] = 1 - st.get("ab", 0)
            return 6 + st["ab"]
        while True:
            i = st["bank"]
            st["bank"] = (i + 1) % 8
            if i not in st.get("reserved", ()):
                return i

    pp = A.alloc([PP_N], F32)
    identf = A.alloc([128], F32)
    identb = A.alloc([128], BF16)
    onesb = A.alloc([128], BF16)
    biasb = A.alloc([12, 640], F32)
    wa_bd = A.alloc([6, 128], BF16)
    wi_bd = A.alloc([6, 128], BF16)
    cfeat = A.alloc([6], F32)
    cfeat2 = A.alloc([6], F32)
    hba = A.alloc([6], F32)
    hbi = A.alloc([6], F32)
    KT = A.alloc([6, SEQ], BF16)
    VP = A.alloc([16, ATT], BF16)
    MKT = A.alloc([2, 2, NMEM], BF16)
    MV = A.alloc([2, 2, MEMW], BF16)
    convst = A.alloc([6, 3], F32)
    lrust = A.alloc([6], F32)

    def body():
        P.dma("sp", pp[:], pp_d[:])
        P.dma("sp", identf[:], ident_d[:])
        for h0 in range(0, 12, 4):
            P.dma("sp", biasb[:, h0:h0 + 4, :], bias_d[:, h0:h0 + 4, :])
        P.dma("pool", wa_bd[:], wa_bd_d[:])
        P.dma("pool", wi_bd[:], wi_bd_d[:])
        P.copy(identb[:], identf[:], eng="dve")
        P.memset(onesb[:], 1.0, eng="pool")
        P.memset(convst[:], 0.0, eng="pool")
        P.memset(lrust[:], 0.0, eng="pool")

        def ppc(c0, n=1):
            return pp[:, c0:c0 + n]

        _stage("pro_consts")

        m0 = A.mark()
        t1 = A.alloc([6], F32)
        t2 = A.alloc([6], F32)
        lam = ppc(PP_LAM, 6)
        P.act(t1[:], lam, AF.Abs)
        P.act(t1[:], t1[:], AF.Exp, scale=-1.0)
        P.act(t1[:], t1[:], AF.Ln, bias=1.0)
        P.ts(t2[:], lam, -1.0, 0.0, ALU.mult, ALU.max)
        P.tt(t2[:], t2[:], t1[:], ALU.add)
        P.ts(cfeat[:], t2[:], -8.0, None, ALU.mult)
        P.ts(cfeat2[:], t2[:], -4.0, None, ALU.mult)
        P.ts(hba[:], ppc(PP_BA, 6), 0.5, None, ALU.mult)
        P.ts(hbi[:], ppc(PP_BI, 6), 0.5, None, ALU.mult)
        A.release(m0)
        _stage("pro_cfeat")

        class _NT:
            track = False
        for_untracked = _NT()

        def UV(ap):
            return V(for_untracked, ap, [])

        wscratch = {}

        def convert_w(key, src, shp):
            n = int(np.prod(shp[1:]))
            t = T(P, "ws_" + key, [128, n], BF16, "dram", kind="Internal", track=True)
            ap = t.h.ap()
            if len(shp) == 3:
                ap = ap.rearrange("p (k n) -> p k n", k=shp[1])
            v = V(t, ap, [(0, 128, 0, n)])
            wscratch[key] = (v, tuple(shp))
            P.dma("pool", v, src)

        def load_w(dst, src, key):
            shp = tuple(dst.ap.shape)
            if key not in wscratch:
                P.dma("pool", dst, src)
                n = int(np.prod(shp[1:]))
                t = T(P, "ws_" + key, [128, n], BF16, "dram", kind="Internal", track=True)
                ap = t.h.ap()
                if len(shp) == 3:
                    ap = ap.rearrange("p (k n) -> p k n", k=shp[1])
                v = V(t, ap, [(0, 128, 0, n)])
                wscratch[key] = (v, shp)
                P.dma("sp", v, dst)
                return
            v, shp0 = wscratch[key]
            assert shp0 == shp, (key, shp0, shp)
            P.dma("sp", dst, v)

        def wview(t, c0, c1, lead=None):
            ap = t.ap()
            if lead is not None:
                ap = ap[lead]
            return UV(ap[:, c0:c1].rearrange("(k p) n -> p k n", p=128))

        def preconvert(group):
            wdv = lambda l, blk: UV(w_dn.ap()[l][:, blk * 256:(blk + 1) * 256].rearrange("(j p) n -> p j n", p=128))

            def ffn(l):
                for blk in range(6):
                    j0 = blk * 4
                    nj = min(4, NJ - j0)
                    convert_w(f"wg{l}_{blk}", wview(w_gu, j0 * 128, (j0 + nj) * 128, lead=l), (128, 8, nj * 128))
                    convert_w(f"wu{l}_{blk}", wview(w_gu, DFF + j0 * 128, DFF + (j0 + nj) * 128, lead=l), (128, 8, nj * 128))
                for blk in range(4):
                    convert_w(f"wd{l}_{blk}", wdv(l, blk), (128, NJ, 256))

            if group == 0:
                for q in range(4):
                    convert_w(f"wA_{q}", wview(w_in_a, q * 448, (q + 1) * 448), (128, 8, 448))
            elif group == 1:
                for n0 in (0, 512):
                    convert_w(f"wo0_{n0}", wview(w_out, n0, n0 + 512, lead=0), (128, 8, 512))
                ffn(0)
            elif group == 2:
                for q in range(3):
                    convert_w(f"wkv_{q}", wview(w_kv, q * 512, (q + 1) * 512), (128, 8, 512))
                for n0 in (0, 512):
                    convert_w(f"wB_{n0}", wview(w_in_b, n0, n0 + 512), (128, 8, 512))
                for n0 in (0, 512):
                    convert_w(f"wo1_{n0}", wview(w_out, n0, n0 + 512, lead=1), (128, 8, 512))
            elif group == 3:
                ffn(1)

        class Stats:
            def __init__(self, xT, nt, sq, rstd):
                self.xT, self.nt, self.sq, self.rstd = xT, nt, sq, rstd
                self.b = nb()
                st.setdefault("reserved", set()).add(self.b)
                self.pending = None
                self.n_mm = 0

            def _mm(self, k, last):
                P.mm(banks[self.b][:, :self.nt], onesb[:], self.sq[:, k, :self.nt], start=(self.n_mm == 0), stop=last)
                self.n_mm += 1

            def feed(self, k):
                P.act(self.sq[:, k, :self.nt], self.xT[:, k, :self.nt], AF.Square)
                if self.pending is not None:
                    self._mm(self.pending, False)
                self.pending = k

            def finish(self):
                self._mm(self.pending, True)
                st["reserved"].discard(self.b)
                P.act(self.rstd[:, :self.nt], banks[self.b][:, :self.nt], AF.Ln, bias=EPS, scale=1.0 / D)
                P.act(self.rstd[:, :self.nt], self.rstd[:, :self.nt], AF.Exp, scale=-0.5)

        def rms_stats(xT, nt, sq, rstd):
            s_ = Stats(xT, nt, sq, rstd)
            for k in range(8):
                s_.feed(k)
            s_.finish()

        def rms_apply(hT, xT, nt, rstd, gcol, tmp=None):
            for k in range(8):
                P.stt(hT[:, k, :nt], xT[:, k, :nt], ppc(gcol + k), rstd[:, :nt], ALU.mult, ALU.mult)

        m0 = A.mark()
        memxT = A.alloc([8, NMEM], F32)
        sqb = A.alloc([8, NMEM], BF16)
        rstd = A.alloc([512], F32)
        hm = A.alloc([8, NMEM], BF16)
        wm = A.alloc([8, 512], BF16)
        stg = A.alloc([512], F32)
        P.dma("sp", memxT[:], UV(mem_p.ap().rearrange("(k p) t -> p k t", p=128)))
        _stage("pro_tr")
        rms_stats(memxT, NMEM, sqb, rstd)
        _stage("pro_stats")
        for l in range(2):
            load_w(wm[:], wview(w_mem_kv, 0, 512, lead=l), f"wm{l}")
            if l == 0:
                preconvert(0)
            else:
                preconvert(1)
            rms_apply(hm, memxT, NMEM, rstd, PP_GMEM + 8 * l)
            for c in range(2):
                b = nb()
                for k in range(8):
                    P.mm(banks[b][:, :NMEM], wm[:, k, c * 128:(c + 1) * 128], hm[:, k, :], start=(k == 0), stop=(k == 7))
                P.copy(MKT[:, l, c, :], banks[b][:, :NMEM], eng="act")
            for mt in range(2):
                b = nb()
                for k in range(8):
                    P.mm(banks[b][:, :], hm[:, k, mt * 128:(mt + 1) * 128], wm[:, k, :], start=(k == 0), stop=(k == 7))
                P.copy(stg[:], banks[b][:], eng="act")
                P.copy(MV[:, l, mt, :], banks[b][:, 256:512], eng="dve")
                P.dma("sp", mk_p[l, mt * 128:(mt + 1) * 128, :], stg[:, 0:256])
                P.dma("sp", mv_p[l, mt * 128:(mt + 1) * 128, :], stg[:, 256:512])
        A.release(m0)

        def run_pass(kind, pi):
            prompt = kind == "p"
            NT = 512 if prompt else NSB * LS
            nseg = 1 if prompt else NSB
            L = NT // nseg
            t0 = pi * 512 if prompt else 0
            ntile = NT // 128
            xsrc = x_p if prompt else x_s
            ydst = y_p if prompt else y_s
            mp = A.mark()
            xT = A.alloc([8, NT], F32)
            hT = A.alloc([8, NT], BF16)
            rstd = A.alloc([NT], F32)
            sqb = A.alloc([8, NT], BF16)
            mixT = A.alloc([8, NT], BF16)
            ntmp = None

            for k in range(8):
                P.dma("sp", xT[:, k, :], UV(xsrc.ap()[k * 128:(k + 1) * 128, t0:t0 + NT]))
            _stage(f"{kind}{pi}_load")
            if not prompt:
                MKs = A.alloc([NSB, 2, 2, NMEM], BF16)
                MVs = A.alloc([NSB, 2, 2, MEMW], BF16)
                cst = A.alloc([6, NSB, 3], F32)
                lst0 = A.alloc([6, NSB], F32)
                for c in range(6):
                    P.dma("sp", cst[:, c, :, :],
                          UV(st_conv.ap()[:, :, c * 128:(c + 1) * 128].rearrange("b t p -> p b t")), **NCD)
                    P.dma("sp", lst0[:, c, :],
                          UV(st_lru.ap()[:, c * 128:(c + 1) * 128].rearrange("b p -> p b")), **NCD)
                m1 = A.mark()
                mkt = A.alloc([2, MEMW], BF16)
                for bi in range(NSB):
                    for l in range(2):
                        P.dma("pool", mkt[:], UV(c_mk.ap()[l, bi].rearrange("(t p) d -> p t d", p=128)))
                        P.dma("pool", MVs[:, bi, l, :, :], UV(c_mv.ap()[l, bi].rearrange("(t p) d -> p t d", p=128)))
                        b = nb()
                        for c in range(2):
                            for mt in range(2):
                                P.transpose(banks_bf[b][:, (c * 2 + mt) * 128:(c * 2 + mt + 1) * 128],
                                            mkt[:, mt, c * 128:(c + 1) * 128], identb[:])
                        mv_ = MKs[:, bi, l, :, :]
                        P.copy(V(mv_.t, MKs.base[:, bi, l].rearrange("p a b -> p (a b)"), mv_.boxes), banks_bf[b][:, 0:512], eng="dve")
                A.release(m1)

            def memK(seg, l):
                return MKT[:, l] if prompt else MKs[:, seg, l]

            class Ring(list):
                def __getitem__(self, c):
                    return list.__getitem__(self, c % len(self))

            def alloc_att():
                return dict(nmx=Ring(A.alloc([1], F32) for _ in range(4)),
                            rsum=Ring(A.alloc([1], F32) for _ in range(4)),
                            rinv=Ring(A.alloc([1], F32) for _ in range(6)),
                            pbf=Ring(A.alloc([640], BF16) for _ in range(4)),
                            ptb=Ring(A.alloc([5, 128], BF16) for _ in range(4)),
                            sb=Ring(A.alloc([640], F32) for _ in range(5)))

            def run_units(units, att):
                n = len(units)

                def kindB(u, U):
                    return U["bias"] is not None and u % 2 == 1

                def s0(u, U):
                    if U.get("pre"):
                        U["pre"]()
                    sp_ = spair[u % 2]
                    nq = U["nq"]
                    groups = U["sparts"]
                    if not isinstance(groups[0], dict):
                        groups = [dict(r0=0, nr=nq, parts=groups)]
                    for g_ in groups:
                        r0, nr = g_["r0"], g_["nr"]
                        col = 0
                        for (qv, kv_, nn) in g_["parts"]:
                            c0 = col
                            while c0 < col + nn:
                                c1 = min(col + nn, (c0 // 512 + 1) * 512)
                                P.mm(sp_[r0:r0 + nr, c0:c1], qv, V(kv_.t, kv_.ap[:, c0 - col:c1 - col], kv_.boxes))
                                c0 = c1
                            col += nn

                def s1(u, U):
                    sp_ = spair[u % 2]
                    nq, nk = U["nq"], U["nk"]
                    sb = att["sb"][u]
                    if U["bias"] is not None and not kindB(u, U):
                        P.tt(sb[:nq, :nk], sp_[:nq, :nk], U["bias"], ALU.add)
                    else:
                        P.copy(sb[:nq, :nk], sp_[:nq, :nk], eng=U.get("evac_eng", "act"))

                def s2(u, U):
                    if kindB(u, U):
                        nq, nk = U["nq"], U["nk"]
                        sb = att["sb"][u]
                        P.tt(sb[:nq, :nk], sb[:nq, :nk], U["bias"], ALU.add, eng="pool")

                def s3(u, U):
                    nq, nk = U["nq"], U["nk"]
                    P.rmax(att["nmx"][u][:nq, :], att["sb"][u][:nq, :nk], negate=True)

                def s4(u, U):
                    nq, nk = U["nq"], U["nk"]
                    P.act(att["pbf"][u][:nq, :nk], att["sb"][u][:nq, :nk], AF.Exp, bias=att["nmx"][u][:nq, :],
                          accum=att["rsum"][u][:nq, :])

                def s5(u, U):
                    nq, nk = U["nq"], U["nk"]
                    pt = banks_bf[4 + u % 2]
                    pbf = att["pbf"][u]
                    nkt = (nk + 127) // 128
                    for kt in range(nkt):
                        w = min(128, nk - kt * 128)
                        P.transpose(pt[:w, kt * 128:kt * 128 + nq], pbf[:nq, kt * 128:kt * 128 + w], identb[:nq, :nq])
                    P.recip(att["rinv"][u][:nq, :], att["rsum"][u][:nq, :])

                def s6(u, U):
                    nq, nk = U["nq"], U["nk"]
                    pt = banks_bf[4 + u % 2]
                    ptb = att["ptb"][u]
                    eng = "dve"
                    nfull = nk // 128
                    if nq == 128:
                        ptv = ptb[:, 0:nfull, :]
                        P.copy(V(ptv.t, ptb.base[:, 0:nfull, :].rearrange("p a b -> p (a b)"), ptv.boxes),
                               pt[:, 0:nfull * 128], eng=eng)
                    else:
                        src = pt[:, 0:nfull * 128]
                        P.copy(ptb[:, 0:nfull, :nq],
                               V(src.t, pt.base[:, 0:nfull * 128].rearrange("p (a b) -> p a b", a=nfull)[:, :, :nq], src.boxes),
                               eng=eng)
                    if nk % 128:
                        w = nk % 128
                        P.copy(ptb[:w, nfull, :nq], pt[:w, nfull * 128:nfull * 128 + nq], eng=eng)

                def s7(u, U):
                    nq = U["nq"]
                    ob = banks[4 + u % 2]
                    ptb = att["ptb"][u]
                    groups = U["vparts"]
                    if not isinstance(groups[0], dict):
                        groups = [dict(r0=0, nr=nq, parts=groups)]
                    for g_ in groups:
                        r0, nr, vp = g_["r0"], g_["nr"], g_["parts"]
                        for kt, (vv, nn) in enumerate(vp):
                            P.mm(ob[r0:r0 + nr, 384:448], ptb[:nn, kt, r0:r0 + nr], vv, start=(kt == 0), stop=(kt == len(vp) - 1))

                def s8(u, U):
                    nq = U["nq"]
                    ob = banks[4 + u % 2]
                    P.act(U["out"], ob[:nq, 384:448], AF.Copy, scale=att["rinv"][u][:nq, :])
                    if U.get("post"):
                        U["post"]()

                stages = (s0, s1, s2, s3, s4, s5, s6, s7, s8)
                st["att"] = True
                try:
                    _run(stages, units, n)
                finally:
                    st["att"] = False

            def _run(stages, units, n):
                ns = len(stages)
                if _DBG.get("noskew"):
                    for u in range(n):
                        for si in range(ns):
                            stages[si](u, units[u])
                    return
                for step in range(n + ns - 1):
                    for si in range(ns):
                        u = step - si
                        if 0 <= u < n:
                            stages[si](u, units[u])

            def mem_units(l, mqT, qoff, nq, seg, osb, ocol, post=None):
                us = []
                for h in range(4):
                    c, po = h // 2, 64 * (h % 2)
                    kt = (MKT[:, l, c, :] if prompt else MKs[:, seg, l, c, :])
                    vparts = [((MV[:, l, mt, h * 64:(h + 1) * 64] if prompt else MVs[:, seg, l, mt, h * 64:(h + 1) * 64]), 128)
                              for mt in range(2)]
                    us.append(dict(nq=nq, sparts=[(mqT[:, h, qoff:qoff + nq], kt, NMEM)], bias=None, nk=NMEM,
                                   vparts=vparts, out=osb[:nq, ocol + h * 64:ocol + (h + 1) * 64],
                                   post=(post if h == 3 else None)))
                return us

            def mem_units_pair(l, g, osb, ocol, post=None):
                us = []
                for h in range(4):
                    c = h // 2
                    sg_, vg_ = [], []
                    for e in range(2):
                        b_ = 2 * g + e
                        sg_.append(dict(r0=64 * e, nr=64, parts=[(mqT[:, h, b_ * L:(b_ + 1) * L], MKs[:, b_, l, c, :], NMEM)]))
                        vg_.append(dict(r0=64 * e, nr=64,
                                        parts=[(MVs[:, b_, l, mt, h * 64:(h + 1) * 64], 128) for mt in range(2)]))
                    us.append(dict(nq=128, sparts=sg_, bias=None, nk=NMEM, vparts=vg_,
                                   out=osb[:, ocol + h * 64:ocol + (h + 1) * 64], post=(post if h == 3 else None)))
                return us

            def out_proj_and_ffn(l):
                m1 = A.mark()
                wo = A.alloc([8, D], BF16)
                for n0 in (0, 512):
                    load_w(wo[:, :, n0:n0 + 512], wview(w_out, n0, n0 + 512, lead=l), f"wo{l}_{n0}")
                sst = Stats(xT, NT, sqb, rstd)
                for j in range(8):
                    b = nb()
                    for k in range(8):
                        P.mm(banks[b][:, :NT], wo[:, k, j * 128:(j + 1) * 128], mixT[:, k, :], start=(k == 0), stop=(k == 7))
                    P.tt(xT[:, j, :], xT[:, j, :], banks[b][:, :NT], ALU.add)
                    sst.feed(j)
                sst.finish()
                A.release(m1)
                m1 = A.mark()
                rms_apply(hT, xT, NT, rstd, PP_GFFN + 8 * l, ntmp)
                actT = A.alloc([NJ, NT], BF16)
                m_gu = A.mark()
                wg = [A.alloc([8, 512], BF16) for _ in range(2)]
                wu = [A.alloc([8, 512], BF16) for _ in range(2)]
                sg = [A.alloc([NT], F32) for _ in range(2)]
                for blk in range(6):
                    j0 = blk * 4
                    nj = min(4, NJ - j0)
                    s = blk % 2
                    load_w(wg[s][:, :, :nj * 128], wview(w_gu, j0 * 128, (j0 + nj) * 128, lead=l), f"wg{l}_{blk}")
                    load_w(wu[s][:, :, :nj * 128], wview(w_gu, DFF + j0 * 128, DFF + (j0 + nj) * 128, lead=l), f"wu{l}_{blk}")
                    for jj in range(nj):
                        j = j0 + jj
                        bg = nb()
                        for k in range(8):
                            P.mm(banks[bg][:, :NT], wg[s][:, k, jj * 128:(jj + 1) * 128], hT[:, k, :], start=(k == 0), stop=(k == 7))
                        bu = nb()
                        for k in range(8):
                            P.mm(banks[bu][:, :NT], wu[s][:, k, jj * 128:(jj + 1) * 128], hT[:, k, :], start=(k == 0), stop=(k == 7))
                        P.act(sg[j % 2][:], banks[bg][:, :NT], AF.Silu)
                        P.tt(actT[:, j, :], sg[j % 2][:], banks[bu][:, :NT], ALU.mult)
                A.release(m_gu)
                wd = [A.alloc([NJ, 256], BF16) for _ in range(2)]
                sst2 = Stats(xT, NT, sqb, rstd)
                for blk in range(4):
                    s = blk % 2
                    load_w(wd[s][:], UV(w_dn.ap()[l][:, blk * 256:(blk + 1) * 256].rearrange("(j p) n -> p j n", p=128)), f"wd{l}_{blk}")
                    for nn in range(2):
                        n = blk * 2 + nn
                        b = nb()
                        for j in range(NJ):
                            P.mm(banks[b][:, :NT], wd[s][:, j, nn * 128:(nn + 1) * 128], actT[:, j, :], start=(j == 0), stop=(j == NJ - 1))
                        P.tt(xT[:, n, :], xT[:, n, :], banks[b][:, :NT], ALU.add)
                        sst2.feed(n)
                sst2.finish()
                A.release(m1)

            mqT = A.alloc([4, NT], BF16)

            def evac_q(dst, c2, bank):
                for e in range(2):
                    P.act(dst[64 * e:64 * e + 64, 2 * c2 + e, :], bank[64 * e:64 * e + 64, :NT], AF.Copy, scale=0.125)

            def zero_q(dst, nh):
                for h_ in range(nh):
                    e = h_ % 2
                    P.memset(dst[64 * (1 - e):64 * (1 - e) + 64, h_, :], 0.0, eng="pool")

            def osb_to_mixT(osb, nq, tcol, c_lo, c_hi):
                b = nb()
                for c in range(c_lo, c_hi):
                    P.transpose(banks_bf[b][:, (c - c_lo) * 128:(c - c_lo) * 128 + nq], osb[:nq, c * 128:(c + 1) * 128], identb[:nq, :nq])
                for c in range(c_lo, c_hi):
                    P.copy(mixT[:, c, tcol:tcol + nq], banks_bf[b][:, (c - c_lo) * 128:(c - c_lo) * 128 + nq],
                           eng=("act" if (c_lo // 4) % 2 else "dve"))

            rms_stats(xT, NT, sqb, rstd)
            rms_apply(hT, xT, NT, rstd, PP_GMIX, ntmp)
            m1 = A.mark()
            wA = A.alloc([8, 1792], BF16)
            for q in range(4):
                load_w(wA[:, :, q * 448:(q + 1) * 448], wview(w_in_a, q * 448, (q + 1) * 448), f"wA_{q}")
            class Ring(list):
                def __getitem__(self, c):
                    return list.__getitem__(self, c % len(self))

            xr = Ring(A.alloc([nseg, L + 3], F32) for _ in range(2))
            gg = Ring(A.alloc([NT], F32) for _ in range(5))
            xc = Ring(A.alloc([nseg, L], F32) for _ in range(4))
            xcb = Ring(A.alloc([nseg, L], BF16) for _ in range(2))
            ra = Ring(A.alloc([nseg, L], F32) for _ in range(2))
            iu = Ring(A.alloc([nseg, L], F32) for _ in range(2))
            a2 = Ring(A.alloc([nseg, L], F32) for _ in range(2))
            hs = Ring(A.alloc([nseg, L], F32) for _ in range(2))
            cout = A.alloc([6, nseg, 3], F32)
            lout = A.alloc([6, nseg], F32)
            lbank = {}

            def as2d(v, buf):
                return V(v.t, buf.base.rearrange("p s l -> p (s l)"), v.boxes)

            def L1(c):
                s = c
                b = nb()
                for k in range(8):
                    P.mm(banks[b][:, :NT], wA[:, k, c * 128:(c + 1) * 128], hT[:, k, :], start=(k == 0), stop=(k == 7))
                b2 = nb()
                for k in range(8):
                    P.mm(banks[b2][:, :NT], wA[:, k, 768 + c * 128:768 + (c + 1) * 128], hT[:, k, :], start=(k == 0), stop=(k == 7))
                lbank[c] = (b, b2)

            def L2(c):
                s = c
                b, b2 = lbank[c]
                if prompt:
                    P.copy(xr[s][:, 0, 0:3], convst[:, c, :], eng="pool")
                    P.copy(xr[s][:, 0, 3:3 + L], banks[b][:, :NT], eng="act")
                else:
                    P.copy(xr[s][:, :, 0:3], cst[:, c, :, :], eng="pool")
                    P.copy(xr[s][:, :, 3:3 + L],
                           V(PS.t, banks[b].base[:, :NT].rearrange("p (s l) -> p s l", s=nseg), banks[b][:, :NT].boxes), eng="act")
                P.act(gg[s][:], banks[b2][:, :NT], AF.Gelu_apprx_tanh)

            def L3(c):
                s = c
                cw = PP_CONVW + c * 4
                P.ts(xc[s][:], xr[s][:, :, 3:3 + L], ppc(cw + 3), ppc(PP_CONVB + c), ALU.mult, ALU.add)
                for kk in range(3):
                    if kk == 2:
                        P.stt(xcb[s][:], xr[s][:, :, kk:kk + L], ppc(cw + kk), xc[s][:], ALU.mult, ALU.add)
                    P.stt(xc[s][:], xr[s][:, :, kk:kk + L], ppc(cw + kk), xc[s][:], ALU.mult, ALU.add)
                if prompt:
                    P.copy(convst[:, c, :], xr[s][:, 0, L:L + 3], eng="pool")
                else:
                    P.copy(cout[:, c, :, :], xr[s][:, :, L:L + 3], eng="pool")

            def L4(c):
                s = c
                xcb2 = as2d(xcb[s][:], xcb[s])
                br = nb()
                P.mm(banks[br][:, :NT], wa_bd[:, c, :], xcb2)
                bi_ = nb()
                P.mm(banks[bi_][:, :NT], wi_bd[:, c, :], xcb2)
                lbank[(c, "g")] = (br, bi_)

            def L5(c):
                s = c
                br, bi_ = lbank[(c, "g")]
                ra2 = as2d(ra[s][:], ra[s])
                iu2 = as2d(iu[s][:], iu[s])
                a22 = as2d(a2[s][:], a2[s])
                P.act(ra2, banks[br][:, :NT], AF.Tanh, bias=hba[:, c:c + 1], scale=0.5)
                P.act(iu2, banks[bi_][:, :NT], AF.Tanh, bias=hbi[:, c:c + 1], scale=0.5)
                P.act(ra2, ra2, AF.Exp, scale=cfeat2[:, c:c + 1], bias=cfeat2[:, c:c + 1])

            def L6(c):
                s = c
                a22 = as2d(a2[s][:], a2[s])
                ra2 = as2d(ra[s][:], ra[s])
                P.ts(a22, ra2, -1.0, 1.0, ALU.mult, ALU.add)
                P.stt(a22, ra2, 1.0, a22, ALU.add, ALU.mult)
                P.act(a22, a22, AF.Sqrt)
                if prompt and pi == 0:
                    P.memset(a2[s][:, 0, 0:1], 1.0, eng="dve")

            def L7(c):
                s = c
                xc2 = as2d(xc[s][:], xc[s])
                iu2 = as2d(iu[s][:], iu[s])
                a22 = as2d(a2[s][:], a2[s])
                P.stt(iu2, iu2, 1.0, xc2, ALU.add, ALU.mult)
                P.stt(iu2, iu2, 0.5, a22, ALU.mult, ALU.mult)
                for sg_ in range(nseg):
                    init = lrust[:, c:c + 1] if prompt else lst0[:, c, sg_:sg_ + 1]
                    P.scan(hs[s][:, sg_, :], ra[s][:, sg_, :], iu[s][:, sg_, :], init)
                if prompt:
                    P.copy(lrust[:, c:c + 1], hs[s][:, 0, L - 1:L], eng="dve")
                else:
                    P.copy(lout[:, c, :], hs[s][:, :, L - 1], eng="dve")
                P.tt(mixT[:, c, :], as2d(hs[s][:], hs[s]), gg[s][:], ALU.mult)

            def L56(c):
                L5(c)
                L6(c)

            lst_ = (L1, L2, L3, L4, L56, L7)
            for step in range(6 + len(lst_) - 1):
                for si, fn in enumerate(lst_):
                    c = step - si
                    if 0 <= c < 6:
                        fn(c)
            for c in range(2):
                b = nb()
                for k in range(8):
                    P.mm(banks[b][:, :NT], wA[:, k, 1536 + c * 128:1536 + (c + 1) * 128], hT[:, k, :], start=(k == 0), stop=(k == 7))
                if c == 0:
                    zero_q(mqT, 4)
                evac_q(mqT, c, banks[b])
            if prompt and pi == 3:
                for c in range(6):
                    P.dma("sp", UV(conv_p.ap()[:, c * 128:(c + 1) * 128].rearrange("t p -> p t")), convst[:, c, :], **NCD)
                    P.dma("sp", UV(lru_p.ap()[c * 128:(c + 1) * 128].rearrange("(p o) -> p o", o=1)), lrust[:, c:c + 1], **NCD)
            if not prompt:
                for c in range(6):
                    P.dma("sp", UV(conv_s.ap()[:, :, c * 128:(c + 1) * 128].rearrange("b t p -> p b t")), cout[:, c, :, :], **NCD)
                    P.dma("sp", UV(lru_s.ap()[:, c * 128:(c + 1) * 128].rearrange("b p -> p b")), lout[:, c, :], **NCD)
            A.release(m1)
            _stage(f"{kind}{pi}_lru")
            if prompt and pi == 0:
                preconvert(2)
            m1 = A.mark()
            att = alloc_att()
            osbs = [A.alloc([D], BF16) for _ in range(2)]
            units = []
            ngrp = ntile
            gq = 128
            for g in range(ngrp):
                ob_ = osbs[g % 2]
                post_ = (lambda ob_=ob_, g=g: osb_to_mixT(ob_, gq, g * gq, 6, 8))
                if prompt:
                    units += mem_units(0, mqT, g * gq, gq, g, ob_, 768, post=post_)
                else:
                    units += mem_units_pair(0, g, ob_, 768, post=post_)
            for U_ in units:
                U_["evac_eng"] = "dve"
            run_units(units, att)
            A.release(m1)
            _stage(f"{kind}{pi}_mem0")
            if prompt and pi == 0:
                preconvert(3)
            out_proj_and_ffn(0)
            _stage(f"{kind}{pi}_ffn0")

            hT2 = A.alloc([8, NT], BF16)
            if not prompt:
                KTs = A.alloc([6, NT], BF16)
                VS = A.alloc([NSB, ATT], BF16)
            m1 = A.mark()
            rms_apply(hT, xT, NT, rstd, PP_GKV, ntmp)
            rms_apply(hT2, xT, NT, rstd, PP_GMIX + 8, ntmp)
            wkv = A.alloc([8, 1536], BF16)
            for q in range(3):
                load_w(wkv[:, :, q * 512:(q + 1) * 512], wview(w_kv, q * 512, (q + 1) * 512), f"wkv_{q}")
            if prompt:
                KTd, kcol = KT, t0
            else:
                KTd, kcol = KTs, 0
            for c in range(6):
                b = nb()
                for k in range(8):
                    P.mm(banks[b][:, :NT], wkv[:, k, c * 128:(c + 1) * 128], hT[:, k, :], start=(k == 0), stop=(k == 7))
                P.copy(KTd[:, c, kcol:kcol + NT], banks[b][:, :NT], eng="act")
            kvst = [A.alloc([1536], F32) for _ in range(2)]
            ntok_tiles = ntile if prompt else nseg
            tl = 128 if prompt else L
            for t in range(ntok_tiles):
                need_out = (not prompt) or (pi == 3)
                s = t % 2
                for (n0, nw) in ((0, 512), (512, 512), (1024, 512)):
                    if n0 < 768 and not need_out:
                        if n0 == 0:
                            continue
                    b = nb()
                    for k in range(8):
                        P.mm(banks[b][:tl, :nw], hT[:, k, t * tl:(t + 1) * tl], wkv[:, k, n0:n0 + nw], start=(k == 0), stop=(k == 7))
                    if need_out:
                        P.copy(kvst[s][:tl, n0:n0 + nw], banks[b][:tl, :nw], eng="act")
                    v0 = max(n0, 768)
                    if v0 < n0 + nw:
                        src = banks[b][:tl, v0 - n0:nw]
                        if prompt:
                            P.copy(VP[:, pi * 4 + t, v0 - 768:n0 + nw - 768], src, eng="dve")
                        else:
                            P.copy(VS[:tl, t, v0 - 768:n0 + nw - 768], src, eng="dve")
                if need_out:
                    if prompt:
                        P.dma("sp", k_p[t * 128:(t + 1) * 128, :], kvst[s][:, 0:768])
                        P.dma("sp", v_p[t * 128:(t + 1) * 128, :], kvst[s][:, 768:1536])
                    else:
                        P.dma("sp", k_s[t * L:(t + 1) * L, :], kvst[s][:L, 0:768])
                        P.dma("sp", v_s[t * L:(t + 1) * L, :], kvst[s][:L, 768:1536])
            A.release(m1)

            _stage(f"{kind}{pi}_kv")
            m1 = A.mark()
            qT = A.alloc([12, NT], BF16)
            zero_q(qT, 12)
            m2 = A.mark()
            wB = A.alloc([8, D], BF16)
            for n0 in (0, 512):
                load_w(wB[:, :, n0:n0 + 512], wview(w_in_b, n0, n0 + 512), f"wB_{n0}")
            for c in range(8):
                b = nb()
                for k in range(8):
                    P.mm(banks[b][:, :NT], wB[:, k, c * 128:(c + 1) * 128], hT2[:, k, :], start=(k == 0), stop=(k == 7))
                if c < 6:
                    evac_q(qT, c, banks[b])
                else:
                    evac_q(mqT, c - 6, banks[b])
            A.release(m2)
            att = alloc_att()
            osbs = [A.alloc([D], BF16) for _ in range(2)]
            units = []
            if prompt:
                for t in range(ntile):
                    J = pi * 4 + t
                    k0 = max(0, 128 * J - 512)
                    k1 = 128 * J + 128
                    nk = k1 - k0
                    ob_ = osbs[t % 2]
                    for h in range(12):
                        c, po = h // 2, 64 * (h % 2)
                        vparts = [(VP[:, k0 // 128 + kt, h * 64:(h + 1) * 64], 128) for kt in range(nk // 128)]
                        units.append(dict(nq=128, sparts=[(qT[:, h, t * 128:(t + 1) * 128], KT[:, c, k0:k1], nk)],
                                          bias=biasb[:, h, 640 - nk:640], nk=nk, vparts=vparts,
                                          out=ob_[:, h * 64:(h + 1) * 64], post=None))

                    def post(ob_=ob_, t=t):
                        osb_to_mixT(ob_, 128, t * 128, 0, 4)
                        osb_to_mixT(ob_, 128, t * 128, 4, 8)
                    units += mem_units(1, mqT, t * 128, 128, 0, ob_, 768, post=post)
                run_units(units, att)
            else:
                kctok = A.alloc([4, ATT], BF16)
                KcT = [A.alloc([6, 512], BF16) for _ in range(2)]
                Vc = [A.alloc([4, ATT], BF16) for _ in range(2)]
                for h0 in range(0, 12, 4):
                    P.dma("sp", biasb[:, h0:h0 + 4, :], bias_s_d[:, h0:h0 + 4, :])

                def load_k(g):
                    for e in range(2):
                        b_ = 2 * g + e
                        P.dma("pool", kctok[:], UV(c_k.ap()[b_].rearrange("(t p) d -> p t d", p=128)))
                        for c in range(6):
                            b = nb()
                            for kt in range(4):
                                P.transpose(banks_bf[b][:, kt * 128:(kt + 1) * 128], kctok[:, kt, c * 128:(c + 1) * 128], identb[:])
                            P.copy(KcT[e][:, c, :], banks_bf[b][:, 0:512], eng=("act" if c % 2 else "dve"))

                def load_v(g):
                    for e in range(2):
                        P.dma("pool", Vc[e][:], UV(c_v.ap()[2 * g + e].rearrange("(t p) d -> p t d", p=128)))

                for g in range(nseg // 2):
                    ob_ = osbs[g % 2]
                    for h in range(12):
                        c = h // 2
                        sg_, vg_ = [], []
                        for e in range(2):
                            b_ = 2 * g + e
                            qv = qT[:, h, b_ * L:(b_ + 1) * L]
                            sg_.append(dict(r0=64 * e, nr=64,
                                            parts=[(qv, KcT[e][:, c, :], 512), (qv, KTs[:, c, b_ * L:(b_ + 1) * L], L)]))
                            vp_ = [(Vc[e][:, kt, h * 64:(h + 1) * 64], 128) for kt in range(4)]
                            vp_.append((VS[:L, b_, h * 64:(h + 1) * 64], L))
                            vg_.append(dict(r0=64 * e, nr=64, parts=vp_))
                        units.append(dict(nq=128, sparts=sg_, bias=biasb[:, h, 0:576], nk=576, vparts=vg_,
                                          out=ob_[:, h * 64:(h + 1) * 64], post=None))

                    def post(ob_=ob_, g=g):
                        osb_to_mixT(ob_, 128, g * 128, 0, 4)
                        osb_to_mixT(ob_, 128, g * 128, 4, 8)
                    units += mem_units_pair(1, g, ob_, 768, post=post)
                units[0]["pre"] = lambda: (load_k(0), load_v(0))
                units[16]["pre"] = lambda: load_k(1)
                units[16 + 7]["pre"] = lambda: load_v(1)
                run_units(units, att)
            A.release(m1)
            _stage(f"{kind}{pi}_attn")
            out_proj_and_ffn(1)
            _stage(f"{kind}{pi}_ffn1")

            m1 = A.mark()
            yT = A.alloc([8, NT], F32)
            for k in range(8):
                P.stt(yT[:, k, :], xT[:, k, :], ppc(PP_GFIN + k), rstd[:, :NT], ALU.mult, ALU.mult)
                P.dma("sp", UV(ydst.ap()[k * 128:(k + 1) * 128, t0:t0 + NT]), yT[:, k, :])
            A.release(m1)
            A.release(mp)
            _stage(f"{kind}{pi}_end")

        _stage("prologue")
        for pi in range(4):
            run_pass("p", pi)
        run_pass("s", 0)

    try:
        body()
    except _Stop:
        pass
    _DBG["P"] = P
    P.build()
    print("arena peak (bf16 elems):", A.peak, "ops:", {e: len(P.ops[e]) for e in ENGS})
    return nc


_NC_CACHE = {}


def _host_layout(inputs):
    f = lambda a: np.ascontiguousarray(np.asarray(a, dtype=np.float32))
    pp = np.zeros((128, PP_N), np.float32)

    def colmajor(v):
        return f(v).reshape(-1, 128).T

    for l in range(2):
        pp[:, PP_GMIX + 8 * l:PP_GMIX + 8 * l + 8] = colmajor(inputs["g_mix"][l])
        pp[:, PP_GFFN + 8 * l:PP_GFFN + 8 * l + 8] = colmajor(inputs["g_ffn"][l])
        pp[:, PP_GMEM + 8 * l:PP_GMEM + 8 * l + 8] = colmajor(inputs["g_mem"][l])
    pp[:, PP_GFIN:PP_GFIN + 8] = colmajor(inputs["g_final"])
    pp[:, PP_GKV:PP_GKV + 8] = colmajor(inputs["g_kv"])
    cw = f(inputs["conv_w"])[0]
    for c in range(6):
        pp[:, PP_CONVW + c * 4:PP_CONVW + c * 4 + 4] = cw[:, c * 128:(c + 1) * 128].T
    pp[:, PP_CONVB:PP_CONVB + 6] = colmajor(f(inputs["conv_b"])[0])
    pp[:, PP_BA:PP_BA + 6] = colmajor(f(inputs["b_rg_a"])[0])
    pp[:, PP_BI:PP_BI + 6] = colmajor(f(inputs["b_rg_i"])[0])
    pp[:, PP_LAM:PP_LAM + 6] = colmajor(f(inputs["lru_lambda"])[0])

    def blockdiag(w):
        w = f(w)[0]
        o = np.zeros((128, 6, 128), np.float32)
        for n in range(12):
            c, r = n // 2, 64 * (n % 2)
            o[r:r + 64, c, r:r + 64] = w[n]
        return o

    rel = f(inputs["rel_bias"])[0]
    qi = np.arange(64)[:, None]
    kj = np.arange(576)[None, :]
    idx = np.clip(512 + qi - kj, -256, 256) + 256
    b64 = rel[idx]
    b64 = np.transpose(b64, (0, 2, 1))
    bias_full = np.full((128, 12, 640), NEG, np.float32)
    bias_full[0:64, :, 0:576] = b64
    bias_full[64:128, :, 64:640] = b64
    bias_s = np.full((128, 12, 640), NEG, np.float32)
    bias_s[0:64, :, 0:576] = b64
    bias_s[64:128, :, 0:576] = b64
    return pp, blockdiag(inputs["w_rg_a"]), blockdiag(inputs["w_rg_i"]), bias_full, bias_s


def kernel(**inputs):
    f = lambda a: np.ascontiguousarray(np.asarray(a, dtype=np.float32))
    if "nc" not in _NC_CACHE:
        _NC_CACHE["nc"] = build_program()
    nc = _NC_CACHE["nc"]
    pp, wa_bd, wi_bd, bias_full, bias_s = _host_layout(inputs)
    ident = np.eye(128, dtype=np.float32)
    shared = {
        "pp": pp, "ident": ident, "bias_full": bias_full, "bias_s": bias_s, "wa_bd": wa_bd, "wi_bd": wi_bd,
        "w_in_a": f(inputs["w_in_a"])[0], "w_kv": f(inputs["w_kv"]), "w_in_b": f(inputs["w_in_b"])[0],
        "w_mem_kv": f(inputs["w_mem_kv"]), "w_out": f(inputs["w_out"]),
        "w_gu": f(inputs["w_ffn_gu"]), "w_dn": f(inputs["w_ffn_down"]),
    }
    xp = f(inputs["x_prompt"])
    xs = f(inputs["x_sample"])
    sc = f(inputs["state_conv"])[0]
    sl = f(inputs["state_lru"])[0]
    ck = f(inputs["cache_k"]).reshape(32, 512, ATT)
    cv = f(inputs["cache_v"]).reshape(32, 512, ATT)
    cmk = f(inputs["cache_mem_k"]).reshape(2, 32, NMEM, MEMW)
    cmv = f(inputs["cache_mem_v"]).reshape(2, 32, NMEM, MEMW)
    mp = f(inputs["mem_prompt"])
    in_maps = []
    for c in range(8):
        sb = slice(4 * c, 4 * c + 4)
        m = dict(shared)
        m.update({
            "x_p": np.ascontiguousarray(xp[c].T), "x_s": np.ascontiguousarray(xs[sb].reshape(NSB * LS, D).T),
            "st_conv": np.ascontiguousarray(sc[sb]), "st_lru": np.ascontiguousarray(sl[sb]),
            "c_k": np.ascontiguousarray(ck[sb]), "c_v": np.ascontiguousarray(cv[sb]),
            "c_mk": np.ascontiguousarray(cmk[:, sb]), "c_mv": np.ascontiguousarray(cmv[:, sb]),
            "mem_p": np.ascontiguousarray(mp[c].T),
        })
        in_maps.append(m)
    res = run_bass_kernel_spmd(nc, in_maps, core_ids=list(range(8)))
    R = res.results
    st = lambda name: np.stack([np.asarray(R[c][name], dtype=np.float32) for c in range(8)])
    y_prompt = np.ascontiguousarray(np.transpose(st("y_p"), (0, 2, 1)))
    y_sample = np.ascontiguousarray(np.transpose(st("y_s"), (0, 2, 1))).reshape(32, LS, D)
    conv_pp = st("conv_p")[None]
    lru_pp = st("lru_p")[None]
    k_pp = st("k_p").reshape(8, 512, 12, 64)
    v_pp = st("v_p").reshape(8, 512, 12, 64)
    mk = np.transpose(st("mk_p"), (1, 0, 2, 3)).reshape(2, 8, NMEM, 4, 64)
    mv = np.transpose(st("mv_p"), (1, 0, 2, 3)).reshape(2, 8, NMEM, 4, 64)
    conv_ss = st("conv_s").reshape(32, 3, LRU)[None]
    lru_ss = st("lru_s").reshape(32, LRU)[None]
    k_ss = st("k_s").reshape(32, LS, 12, 64)
    v_ss = st("v_s").reshape(32, LS, 12, 64)
    return (y_prompt, y_sample, conv_pp, lru_pp, k_pp, v_pp,
            np.ascontiguousarray(mk), np.ascontiguousarray(mv), conv_ss, lru_ss, k_ss, v_ss)
```

```python
import numpy as np
from contextlib import ExitStack
import concourse.bass as bass
import concourse.mybir as mybir
from concourse.bass_utils import run_bass_kernel_spmd

F32 = mybir.dt.float32
BF16 = mybir.dt.bfloat16
AF = mybir.ActivationFunctionType
ALU = mybir.AluOpType
AX = mybir.AxisListType

ENGS = ("pe", "act", "dve", "pool", "sp")
SAME_ENGINE_WINDOW = {"pe": 0, "act": 1 << 30, "dve": 1 << 30, "pool": 1 << 30, "sp": 0}
BUCKET = 1024

D = 1024
SEQ = 2048
NSB = 4
LS = 64
LRU = 768
ATT = 768
MEMW = 256
NMEM = 256
DFF = 2816
NJ = DFF // 128
EPS = 1e-6
NEG = -30000.0
NCD = {"allow_slow_non_contiguous": True}


class V:
    __slots__ = ("t", "ap", "boxes")

    def __init__(self, t, ap, boxes):
        self.t = t
        self.ap = ap
        self.boxes = boxes


def _norm_idx(shape, idx):
    if not isinstance(idx, tuple):
        idx = (idx,)
    idx = list(idx) + [slice(None)] * (len(shape) - len(idx))
    out = []
    for d, i in enumerate(idx):
        if isinstance(i, slice):
            s = 0 if i.start is None else i.start
            e = shape[d] if i.stop is None else i.stop
            assert i.step in (None, 1)
            assert 0 <= s < e <= shape[d], (shape, idx)
            out.append((s, e))
        else:
            assert 0 <= int(i) < shape[d], (shape, idx)
            out.append((int(i), int(i) + 1))
    return out


def _boxes(shape, nidx):
    p0, p1 = nidx[0]
    free = nidx[1:]
    fshape = list(shape[1:])
    strides = [int(np.prod(fshape[i + 1:])) for i in range(len(fshape))]

    def rec(d, base):
        if d == len(free):
            return [(base, base + 1)]
        if all(free[j] == (0, fshape[j]) for j in range(d + 1, len(free))):
            s, e = free[d]
            return [(base + s * strides[d], base + e * strides[d])]
        out = []
        for i in range(free[d][0], free[d][1]):
            out += rec(d + 1, base + i * strides[d])
        return out

    return [(p0, p1, a, b) for (a, b) in rec(0, 0)]


class T:
    def __init__(self, P, name, shape, dtype, space="sbuf", kind=None, track=True):
        self.name = name
        self.shape = tuple(shape)
        self.dtype = dtype
        self.space = space
        self.track = track
        nc = P.nc
        if space == "sbuf":
            self.h = P.es.enter_context(nc.sbuf_tensor(name, list(shape), dtype))
        elif space == "psum":
            self.h = P.es.enter_context(nc.psum_tensor(name, list(shape), dtype))
        else:
            self.h = nc.dram_tensor(name, list(shape), dtype, kind=kind)
        self.recs = {}

    def __getitem__(self, idx):
        nidx = _norm_idx(self.shape, idx)
        ap = self.h[idx] if self.space != "dram" else self.h.ap()[idx]
        return V(self, ap, _boxes(self.shape, nidx) if self.track else [])

    def ap(self):
        return self.h.ap()


_ESZ = {F32: 4, BF16: 2}


class Buf:
    def __init__(self, arena, off, fshape, dtype, unit):
        self.arena = arena
        self.off = off
        self.fshape = tuple(fshape)
        self.shape = (128,) + self.fshape
        self.dtype = dtype
        n = int(np.prod(fshape))
        self.scale_num = _ESZ[dtype]
        self.unit = unit
        nun = n * _ESZ[dtype] // unit
        assert n * _ESZ[dtype] % unit == 0
        self.nun = nun
        ap = arena.h[:, off:off + nun]
        if dtype != arena.dtype:
            ap = ap.bitcast(dtype)
        if len(fshape) > 1:
            names = [chr(ord("a") + i) for i in range(len(fshape))]
            kw = {nm: int(s) for nm, s in zip(names[:-1], fshape[:-1])}
            ap = ap.rearrange("p (" + " ".join(names) + ") -> p " + " ".join(names), **kw)
        self.base = ap

    def __getitem__(self, idx):
        nidx = _norm_idx(self.shape, idx)
        bx = _boxes(self.shape, nidx)
        es = _ESZ[self.dtype]
        boxes = [(p0, p1, self.off + (a * es) // self.unit, self.off + (b * es + self.unit - 1) // self.unit)
                 for (p0, p1, a, b) in bx]
        if self.arena.space == "psum":
            boxes = [(0, 128, (a // 512) * 512, ((b + 511) // 512) * 512) for (_, _, a, b) in boxes]
        return V(self.arena, self.base[idx], boxes)


class Arena:
    def __init__(self, P, name, n, dtype, space):
        self.t = T(P, name, [128, n], dtype, space)
        self.n = n
        self.top = 0
        self.unit = _ESZ[dtype]
        self.peak = 0

    def alloc(self, fshape, dtype):
        n = int(np.prod(fshape)) * _ESZ[dtype] // self.unit
        n_al = (n + 15) // 16 * 16
        off = self.top
        self.top += n_al
        self.peak = max(self.peak, self.top)
        assert self.top <= self.n, f"arena overflow {self.top} > {self.n}"
        return Buf(self.t, off, fshape, dtype, self.unit)

    def mark(self):
        return self.top

    def release(self, m):
        self.top = m


class Op:
    __slots__ = ("eng", "fn", "waits", "idx", "tok", "is_dma", "multi")


def _ovl(a, b):
    return a[0] < b[1] and b[0] < a[1] and a[2] < b[3] and b[2] < a[3]


def _cov(a, b):
    return a[0] <= b[0] and a[1] >= b[1] and a[2] <= b[2] and a[3] >= b[3]


class Prog:
    def __init__(self, nc):
        self.nc = nc
        self.es = ExitStack()
        self.ops = {e: [] for e in ENGS}
        self.count = {e: 0 for e in ENGS}
        self.waited = {e: {} for e in ENGS}
        self.needed = set()
        self.n_dma_sems = {"sp": 40, "pool": 40, "act": 8}
        self.dma_rr = {q: 0 for q in self.n_dma_sems}
        self.dma_cnt = {}
        self.final_dma = {}

    def _deps(self, reads, writes, ekey=None):
        deps = {}
        for (vs, only_w) in ((reads, True), (writes, False)):
            for v in vs:
                if not v.t.track:
                    continue
                recs = v.t.recs
                excl = v.t.space == "psum"
                for b in v.boxes:
                    for bk in range(b[2] // BUCKET, (b[3] - 1) // BUCKET + 1):
                        for (rb, key, val, w) in recs.get(bk, ()):
                            if (w or not only_w or (excl and key != ekey)) and _ovl(rb, b):
                                if deps.get(key, 0) < val:
                                    deps[key] = val
        return deps

    def _record(self, reads, writes, key, val):
        for v in writes:
            if not v.t.track:
                continue
            recs = v.t.recs
            for b in v.boxes:
                for bk in range(b[2] // BUCKET, (b[3] - 1) // BUCKET + 1):
                    lst = recs.get(bk)
                    if lst is None:
                        recs[bk] = [(b, key, val, True)]
                    else:
                        lst[:] = [r for r in lst if not _cov(b, r[0])]
                        lst.append((b, key, val, True))
        for v in reads:
            if not v.t.track:
                continue
            recs = v.t.recs
            for b in v.boxes:
                for bk in range(b[2] // BUCKET, (b[3] - 1) // BUCKET + 1):
                    lst = recs.get(bk)
                    if lst is None:
                        recs[bk] = [(b, key, val, False)]
                    else:
                        lst[:] = [r for r in lst if not (not r[3] and r[1] == key and r[0] == b)]
                        lst.append((b, key, val, False))

    def op(self, eng, fn, reads=(), writes=(), dma=False):
        reads = [r for r in reads if isinstance(r, V)]
        writes = [w for w in writes if isinstance(w, V)]
        deps = self._deps(reads, writes, ("eng", eng))
        o = Op()
        o.eng = eng
        o.fn = fn
        o.is_dma = dma
        waits = []
        if dma:
            pool = self.n_dma_sems[eng]
            si = self.dma_rr[eng]
            self.dma_rr[eng] = (si + 1) % pool
            key = ("dma", eng, si)
            prev = self.dma_cnt.get(key, 0)
            if prev:
                deps[key] = max(deps.get(key, 0), prev)
            val = prev + 16
            self.dma_cnt[key] = val
            self.final_dma[key] = val
            o.tok = (key, val)
            o.idx = None
        else:
            self.count[eng] += 1
            o.idx = self.count[eng]
            key = ("eng", eng)
            o.tok = (key, o.idx)
        wd = self.waited[eng]
        vcs = self.__dict__.setdefault("vcs", {})
        for k, v in sorted(deps.items(), key=lambda kv: -kv[1]):
            if k == ("eng", eng) and not dma:
                if (o.idx - v) > SAME_ENGINE_WINDOW[eng]:
                    continue
            if wd.get(k, 0) >= v:
                continue
            wd[k] = v
            waits.append((k, v))
            if k[0] == "eng":
                self.needed.add((k[1], v))
            pv = vcs.get((k, v))
            if pv:
                for k2, v2 in pv.items():
                    if wd.get(k2, 0) < v2:
                        wd[k2] = v2
        o.waits = waits
        snap = dict(wd)
        if not dma:
            snap.pop(("eng", eng), None)
            own = wd.get(("eng", eng), 0)
            if own:
                snap[("eng", eng)] = own
        vcs[o.tok] = snap
        self._record(reads, writes, o.tok[0], o.tok[1])
        self.ops[eng].append(o)
        return o

    def build(self, final_wait_eng="sp"):
        nc = self.nc
        es = self.es
        fo = Op()
        fo.eng = final_wait_eng
        fo.fn = None
        fo.is_dma = False
        fo.idx = None
        fo.waits = []
        for k, v in self.final_dma.items():
            if self.waited[final_wait_eng].get(k, 0) < v:
                fo.waits.append((k, v))
        for e in ENGS:
            if e != final_wait_eng and self.count[e] > 0:
                fo.waits.append((("eng", e), self.count[e]))
                self.needed.add((e, self.count[e]))
        self.ops[final_wait_eng].append(fo)
        rank = {}
        for e in ENGS:
            r = 0
            for o in self.ops[e]:
                if o.idx is not None and (e, o.idx) in self.needed:
                    r += 1
                    rank[(e, o.idx)] = r
        sems = {}
        for e in ENGS:
            sems[("eng", e)] = es.enter_context(nc.semaphore("s_" + e))
        for q, n in self.n_dma_sems.items():
            for i in range(n):
                if ("dma", q, i) in self.dma_cnt:
                    sems[("dma", q, i)] = es.enter_context(nc.semaphore(f"d_{q}_{i}"))
        block = es.enter_context(nc.Block())
        needed = self.needed

        def emit(e, h):
            for o in self.ops[e]:
                waits = [(sems[k], (rank[(k[1], v)] if k[0] == "eng" else v)) for (k, v) in o.waits]
                attach = None
                if (waits and o.fn is not None and not o.is_dma and e in ("act", "dve", "pool")
                        and not getattr(o, "multi", False)):
                    attach = waits.pop()
                for (sm, val) in waits:
                    h.wait_ge(sm, val)
                if o.fn is None:
                    continue
                ins = o.fn(h)
                if attach is not None:
                    ins._wait_ge(attach[0], attach[1])
                if o.is_dma:
                    ins.then_inc(sems[o.tok[0]], 16)
                elif (e, o.idx) in needed:
                    ins.then_inc(sems[("eng", e)], 1)

        @block.sync
        def _(h):
            emit("sp", h)

        @block.scalar
        def _(h):
            emit("act", h)

        @block.vector
        def _(h):
            emit("dve", h)

        @block.gpsimd
        def _(h):
            emit("pool", h)

        @block.tensor
        def _(h):
            emit("pe", h)

    def mm(self, out, lhsT, rhs, start=True, stop=True):
        return self.op("pe", lambda h: h.matmul(out.ap, lhsT.ap, rhs.ap, start=start, stop=stop),
                       reads=[lhsT, rhs], writes=[out])

    def transpose(self, out, in_, ident):
        return self.op("pe", lambda h: h.transpose(out.ap, in_.ap, ident.ap),
                       reads=[in_, ident], writes=[out])

    def act(self, out, in_, func, bias=None, scale=None, accum=None):
        kw = {}
        rd = [in_]
        wr = [out]
        if bias is not None:
            kw["bias"] = bias.ap if isinstance(bias, V) else bias
            rd.append(bias)
        if scale is not None:
            kw["scale"] = scale.ap if isinstance(scale, V) else scale
            rd.append(scale)
        if accum is not None:
            kw["accum_out"] = accum.ap
            wr.append(accum)
        o = self.op("act", lambda h: h.activation(out.ap, in_.ap, func, **kw), reads=rd, writes=wr)
        o.multi = accum is not None
        return o

    def tt(self, out, in0, in1, op, eng="dve"):
        return self.op(eng, lambda h: h.tensor_tensor(out.ap, in0.ap, in1.ap, op),
                       reads=[in0, in1], writes=[out])

    def ts(self, out, in0, s1, s2, op0, op1=None, eng="dve"):
        rd = [in0, s1, s2]
        a1 = s1.ap if isinstance(s1, V) else s1
        a2 = s2.ap if isinstance(s2, V) else s2
        kw = {}
        if op1 is not None:
            kw["op1"] = op1
        return self.op(eng, lambda h: h.tensor_scalar(out.ap, in0.ap, a1, a2, op0, **kw), reads=rd, writes=[out])

    def stt(self, out, in0, s, in1, op0, op1, eng="dve"):
        a = s.ap if isinstance(s, V) else s
        return self.op(eng, lambda h: h.scalar_tensor_tensor(out.ap, in0.ap, a, in1.ap, op0, op1),
                       reads=[in0, s, in1], writes=[out])

    def copy(self, out, in_, eng="dve"):
        if eng == "act":
            return self.op(eng, lambda h: h.copy(out.ap, in_.ap), reads=[in_], writes=[out])
        return self.op(eng, lambda h: h.tensor_copy(out.ap, in_.ap), reads=[in_], writes=[out])

    def memset(self, out, val, eng="pool"):
        return self.op(eng, lambda h: h.memset(out.ap, val), writes=[out])

    def rmax(self, out, in_, negate=False, eng="dve"):
        return self.op(eng, lambda h: h.tensor_reduce(out.ap, in_.ap, AX.X, ALU.max, negate=negate),
                       reads=[in_], writes=[out])

    def recip(self, out, in_, eng="dve"):
        return self.op(eng, lambda h: h.reciprocal(out.ap, in_.ap), reads=[in_], writes=[out])

    def scan(self, out, d0, d1, init):
        a = init.ap if isinstance(init, V) else init
        return self.op("dve", lambda h: h.tensor_tensor_scan(out.ap, d0.ap, d1.ap, a, ALU.mult, ALU.add),
                       reads=[d0, d1, init], writes=[out])

    def dma(self, q, out, in_, **kw):
        return self.op(q, lambda h: h.dma_start(out=out.ap, in_=in_.ap, **kw),
                       reads=[in_], writes=[out], dma=True)


PP_GMIX, PP_GFFN, PP_GFIN, PP_GKV, PP_GMEM = 0, 16, 32, 40, 48
PP_CONVW, PP_CONVB, PP_BA, PP_BI, PP_LAM = 64, 88, 94, 100, 106
PP_N = 112


class _Stop(Exception):
    pass


_DBG = {"stop": None}


def _stage(name):
    if _DBG["stop"] == name:
        raise _Stop()


def build_program():
    nc = bass.Bass("TRN2", target_bir_lowering=False)
    P = Prog(nc)

    def din(name, shape):
        return T(P, name, shape, F32, "dram", kind="ExternalInput", track=False)

    def dout(name, shape):
        return T(P, name, shape, F32, "dram", kind="ExternalOutput", track=False)

    x_p = din("x_p", [D, SEQ])
    x_s = din("x_s", [D, NSB * LS])
    st_conv = din("st_conv", [NSB, 3, LRU])
    st_lru = din("st_lru", [NSB, LRU])
    c_k = din("c_k", [NSB, 512, ATT])
    c_v = din("c_v", [NSB, 512, ATT])
    c_mk = din("c_mk", [2, NSB, NMEM, MEMW])
    c_mv = din("c_mv", [2, NSB, NMEM, MEMW])
    mem_p = din("mem_p", [D, NMEM])
    pp_d = din("pp", [128, PP_N])
    ident_d = din("ident", [128, 128])
    bias_d = din("bias_full", [128, 12, 640])
    bias_s_d = din("bias_s", [128, 12, 640])
    wa_bd_d = din("wa_bd", [128, 6, 128])
    wi_bd_d = din("wi_bd", [128, 6, 128])
    w_in_a = din("w_in_a", [D, 1792])
    w_kv = din("w_kv", [D, 1536])
    w_in_b = din("w_in_b", [D, D])
    w_mem_kv = din("w_mem_kv", [2, D, 512])
    w_out = din("w_out", [2, D, D])
    w_gu = din("w_gu", [2, D, 2 * DFF])
    w_dn = din("w_dn", [2, DFF, D])

    y_p = dout("y_p", [D, SEQ])
    y_s = dout("y_s", [D, NSB * LS])
    conv_p = dout("conv_p", [3, LRU])
    lru_p = dout("lru_p", [LRU])
    k_p = dout("k_p", [512, ATT])
    v_p = dout("v_p", [512, ATT])
    mk_p = dout("mk_p", [2, NMEM, MEMW])
    mv_p = dout("mv_p", [2, NMEM, MEMW])
    conv_s = dout("conv_s", [NSB, 3, LRU])
    lru_s = dout("lru_s", [NSB, LRU])
    k_s = dout("k_s", [NSB * LS, ATT])
    v_s = dout("v_s", [NSB * LS, ATT])

    A = Arena(P, "arena", 104000, BF16, "sbuf")
    PS = Arena(P, "psum", 4096, F32, "psum")
    banks = [PS.alloc([512], F32) for _ in range(8)]
    banks_bf = [Buf(PS.t, i * 512, [1024], BF16, 4) for i in range(8)]
    spair = [Buf(PS.t, 0, [1024], F32, 4), Buf(PS.t, 2 * 512, [1024], F32, 4)]
    st = {"bank": 0, "sp": 0}

    def nb():
        if st.get("att"):
            st["ab"] = 1 - st.get("ab", 0)
            return 6 + st["ab"]
        while True:
            i = st["bank"]
            st["bank"] = (i + 1) % 8
            if i not in st.get("reserved", ()):
                return i

    pp = A.alloc([PP_N], F32)
    identf = A.alloc([128], F32)
    identb = A.alloc([128], BF16)
    onesb = A.alloc([128], BF16)
    biasb = A.alloc([12, 640], F32)
    wa_bd = A.alloc([6, 128], BF16)
    wi_bd = A.alloc([6, 128], BF16)
    cfeat = A.alloc([6], F32)
    cfeat2 = A.alloc([6], F32)
    hba = A.alloc([6], F32)
    hbi = A.alloc([6], F32)
    KT = A.alloc([6, SEQ], BF16)
    VP = A.alloc([16, ATT], BF16)
    MKT = A.alloc([2, 2, NMEM], BF16)
    MV = A.alloc([2, 2, MEMW], BF16)
    convst = A.alloc([6, 3], F32)
    lrust = A.alloc([6], F32)

    def body():
        P.dma("sp", pp[:], pp_d[:])
        P.dma("sp", identf[:], ident_d[:])
        for h0 in range(0, 12, 4):
            P.dma("sp", biasb[:, h0:h0 + 4, :], bias_d[:, h0:h0 + 4, :])
        P.dma("pool", wa_bd[:], wa_bd_d[:])
        P.dma("pool", wi_bd[:], wi_bd_d[:])
        P.copy(identb[:], identf[:], eng="dve")
        P.memset(onesb[:], 1.0, eng="pool")
        P.memset(convst[:], 0.0, eng="pool")
        P.memset(lrust[:], 0.0, eng="pool")

        def ppc(c0, n=1):
            return pp[:, c0:c0 + n]

        _stage("pro_consts")

        m0 = A.mark()
        t1 = A.alloc([6], F32)
        t2 = A.alloc([6], F32)
        lam = ppc(PP_LAM, 6)
        P.act(t1[:], lam, AF.Abs)
        P.act(t1[:], t1[:], AF.Exp, scale=-1.0)
        P.act(t1[:], t1[:], AF.Ln, bias=1.0)
        P.ts(t2[:], lam, -1.0, 0.0, ALU.mult, ALU.max)
        P.tt(t2[:], t2[:], t1[:], ALU.add)
        P.ts(cfeat[:], t2[:], -8.0, None, ALU.mult)
        P.ts(cfeat2[:], t2[:], -4.0, None, ALU.mult)
        P.ts(hba[:], ppc(PP_BA, 6), 0.5, None, ALU.mult)
        P.ts(hbi[:], ppc(PP_BI, 6), 0.5, None, ALU.mult)
        A.release(m0)
        _stage("pro_cfeat")

        class _NT:
            track = False
        for_untracked = _NT()

        def UV(ap):
            return V(for_untracked, ap, [])

        wscratch = {}

        def convert_w(key, src, shp):
            n = int(np.prod(shp[1:]))
            t = T(P, "ws_" + key, [128, n], BF16, "dram", kind="Internal", track=True)
            ap = t.h.ap()
            if len(shp) == 3:
                ap = ap.rearrange("p (k n) -> p k n", k=shp[1])
            v = V(t, ap, [(0, 128, 0, n)])
            wscratch[key] = (v, tuple(shp))
            P.dma("pool", v, src)

        def load_w(dst, src, key):
            shp = tuple(dst.ap.shape)
            if key not in wscratch:
                P.dma("pool", dst, src)
                n = int(np.prod(shp[1:]))
                t = T(P, "ws_" + key, [128, n], BF16, "dram", kind="Internal", track=True)
                ap = t.h.ap()
                if len(shp) == 3:
                    ap = ap.rearrange("p (k n) -> p k n", k=shp[1])
                v = V(t, ap, [(0, 128, 0, n)])
                wscratch[key] = (v, shp)
                P.dma("sp", v, dst)
                return
            v, shp0 = wscratch[key]
            assert shp0 == shp, (key, shp0, shp)
            P.dma("sp", dst, v)

        def wview(t, c0, c1, lead=None):
            ap = t.ap()
            if lead is not None:
                ap = ap[lead]
            return UV(ap[:, c0:c1].rearrange("(k p) n -> p k n", p=128))

        def preconvert(group):
            wdv = lambda l, blk: UV(w_dn.ap()[l][:, blk * 256:(blk + 1) * 256].rearrange("(j p) n -> p j n", p=128))

            def ffn(l):
                for blk in range(6):
                    j0 = blk * 4
                    nj = min(4, NJ - j0)
                    convert_w(f"wg{l}_{blk}", wview(w_gu, j0 * 128, (j0 + nj) * 128, lead=l), (128, 8, nj * 128))
                    convert_w(f"wu{l}_{blk}", wview(w_gu, DFF + j0 * 128, DFF + (j0 + nj) * 128, lead=l), (128, 8, nj * 128))
                for blk in range(4):
                    convert_w(f"wd{l}_{blk}", wdv(l, blk), (128, NJ, 256))

            if group == 0:
                for q in range(4):
                    convert_w(f"wA_{q}", wview(w_in_a, q * 448, (q + 1) * 448), (128, 8, 448))
            elif group == 1:
                for n0 in (0, 512):
                    convert_w(f"wo0_{n0}", wview(w_out, n0, n0 + 512, lead=0), (128, 8, 512))
                ffn(0)
            elif group == 2:
                for q in range(3):
                    convert_w(f"wkv_{q}", wview(w_kv, q * 512, (q + 1) * 512), (128, 8, 512))
                for n0 in (0, 512):
                    convert_w(f"wB_{n0}", wview(w_in_b, n0, n0 + 512), (128, 8, 512))
                for n0 in (0, 512):
                    convert_w(f"wo1_{n0}", wview(w_out, n0, n0 + 512, lead=1), (128, 8, 512))
            elif group == 3:
                ffn(1)

        class Stats:
            def __init__(self, xT, nt, sq, rstd):
                self.xT, self.nt, self.sq, self.rstd = xT, nt, sq, rstd
                self.b = nb()
                st.setdefault("reserved", set()).add(self.b)
                self.pending = None
                self.n_mm = 0

            def _mm(self, k, last):
                P.mm(banks[self.b][:, :self.nt], onesb[:], self.sq[:, k, :self.nt], start=(self.n_mm == 0), stop=last)
                self.n_mm += 1

            def feed(self, k):
                P.act(self.sq[:, k, :self.nt], self.xT[:, k, :self.nt], AF.Square)
                if self.pending is not None:
                    self._mm(self.pending, False)
                self.pending = k

            def finish(self):
                self._mm(self.pending, True)
                st["reserved"].discard(self.b)
                P.act(self.rstd[:, :self.nt], banks[self.b][:, :self.nt], AF.Ln, bias=EPS, scale=1.0 / D)
                P.act(self.rstd[:, :self.nt], self.rstd[:, :self.nt], AF.Exp, scale=-0.5)

        def rms_stats(xT, nt, sq, rstd):
            s_ = Stats(xT, nt, sq, rstd)
            for k in range(8):
                s_.feed(k)
            s_.finish()

        def rms_apply(hT, xT, nt, rstd, gcol, tmp=None):
            for k in range(8):
                P.stt(hT[:, k, :nt], xT[:, k, :nt], ppc(gcol + k), rstd[:, :nt], ALU.mult, ALU.mult)

        m0 = A.mark()
        memxT = A.alloc([8, NMEM], F32)
        sqb = A.alloc([8, NMEM], BF16)
        rstd = A.alloc([512], F32)
        hm_ = [A.alloc([8, NMEM], BF16) for _ in range(2)]
        wm_ = [A.alloc([8, 512], BF16) for _ in range(2)]
        stg_ = [A.alloc([512], F32) for _ in range(4)]
        P.dma("sp", memxT[:], UV(mem_p.ap().rearrange("(k p) t -> p k t", p=128)))
        _stage("pro_tr")
        rms_stats(memxT, NMEM, sqb, rstd)
        _stage("pro_stats")
        for l in range(2):
            hm, wm = hm_[l], wm_[l]
            load_w(wm[:], wview(w_mem_kv, 0, 512, lead=l), f"wm{l}")
            if l == 0:
                preconvert(0)
            else:
                preconvert(1)
            rms_apply(hm, memxT, NMEM, rstd, PP_GMEM + 8 * l)
            for c in range(2):
                b = nb()
                for k in range(8):
                    P.mm(banks[b][:, :NMEM], wm[:, k, c * 128:(c + 1) * 128], hm[:, k, :], start=(k == 0), stop=(k == 7))
                P.copy(MKT[:, l, c, :], banks[b][:, :NMEM], eng="act")
            for mt in range(2):
                b = nb()
                for k in range(8):
                    P.mm(banks[b][:, :], hm[:, k, mt * 128:(mt + 1) * 128], wm[:, k, :], start=(k == 0), stop=(k == 7))
                stg = stg_[2 * l + mt]
                P.copy(stg[:], banks[b][:], eng="act")
                P.copy(MV[:, l, mt, :], banks[b][:, 256:512], eng="dve")
                P.dma("sp", mk_p[l, mt * 128:(mt + 1) * 128, :], stg[:, 0:256])
                P.dma("sp", mv_p[l, mt * 128:(mt + 1) * 128, :], stg[:, 256:512])
        A.release(m0)

        def run_pass(kind, pi):
            prompt = kind == "p"
            NT = 512 if prompt else NSB * LS
            nseg = 1 if prompt else NSB
            L = NT // nseg
            t0 = pi * 512 if prompt else 0
            ntile = NT // 128
            xsrc = x_p if prompt else x_s
            ydst = y_p if prompt else y_s
            mp = A.mark()
            xT = A.alloc([8, NT], F32)
            hT = A.alloc([8, NT], BF16)
            rstd = A.alloc([NT], F32)
            sqb = A.alloc([8, NT], BF16)
            mixT = A.alloc([8, NT], BF16)
            ntmp = None

            for k in range(8):
                P.dma("sp", xT[:, k, :], UV(xsrc.ap()[k * 128:(k + 1) * 128, t0:t0 + NT]))
            _stage(f"{kind}{pi}_load")
            if not prompt:
                MKs = A.alloc([NSB, 2, 2, NMEM], BF16)
                MVs = A.alloc([NSB, 2, 2, MEMW], BF16)
                cst = A.alloc([6, NSB, 3], F32)
                lst0 = A.alloc([6, NSB], F32)
                for c in range(6):
                    P.dma("sp", cst[:, c, :, :],
                          UV(st_conv.ap()[:, :, c * 128:(c + 1) * 128].rearrange("b t p -> p b t")), **NCD)
                    P.dma("sp", lst0[:, c, :],
                          UV(st_lru.ap()[:, c * 128:(c + 1) * 128].rearrange("b p -> p b")), **NCD)
                m1 = A.mark()
                mkt = A.alloc([2, MEMW], BF16)
                for bi in range(NSB):
                    for l in range(2):
                        P.dma("pool", mkt[:], UV(c_mk.ap()[l, bi].rearrange("(t p) d -> p t d", p=128)))
                        P.dma("pool", MVs[:, bi, l, :, :], UV(c_mv.ap()[l, bi].rearrange("(t p) d -> p t d", p=128)))
                        b = nb()
                        for c in range(2):
                            for mt in range(2):
                                P.transpose(banks_bf[b][:, (c * 2 + mt) * 128:(c * 2 + mt + 1) * 128],
                                            mkt[:, mt, c * 128:(c + 1) * 128], identb[:])
                        mv_ = MKs[:, bi, l, :, :]
                        P.copy(V(mv_.t, MKs.base[:, bi, l].rearrange("p a b -> p (a b)"), mv_.boxes), banks_bf[b][:, 0:512], eng="dve")
                A.release(m1)

            def memK(seg, l):
                return MKT[:, l] if prompt else MKs[:, seg, l]

            class Ring(list):
                def __getitem__(self, c):
                    return list.__getitem__(self, c % len(self))

            def alloc_att():
                return dict(nmx=Ring(A.alloc([1], F32) for _ in range(4)),
                            rsum=Ring(A.alloc([1], F32) for _ in range(4)),
                            rinv=Ring(A.alloc([1], F32) for _ in range(6)),
                            pbf=Ring(A.alloc([640], BF16) for _ in range(4)),
                            ptb=Ring(A.alloc([5, 128], BF16) for _ in range(4)),
                            sb=Ring(A.alloc([640], F32) for _ in range(5)))

            def run_units(units, att):
                n = len(units)

                def kindB(u, U):
                    return U["bias"] is not None and u % 2 == 1

                def s0(u, U):
                    if U.get("pre"):
                        U["pre"]()
                    sp_ = spair[u % 2]
                    nq = U["nq"]
                    groups = U["sparts"]
                    if not isinstance(groups[0], dict):
                        groups = [dict(r0=0, nr=nq, parts=groups)]
                    for g_ in groups:
                        r0, nr = g_["r0"], g_["nr"]
                        col = 0
                        for (qv, kv_, nn) in g_["parts"]:
                            c0 = col
                            while c0 < col + nn:
                                c1 = min(col + nn, (c0 // 512 + 1) * 512)
                                P.mm(sp_[r0:r0 + nr, c0:c1], qv, V(kv_.t, kv_.ap[:, c0 - col:c1 - col], kv_.boxes))
                                c0 = c1
                            col += nn

                def s1(u, U):
                    sp_ = spair[u % 2]
                    nq, nk = U["nq"], U["nk"]
                    sb = att["sb"][u]
                    if U["bias"] is not None and not kindB(u, U):
                        P.tt(sb[:nq, :nk], sp_[:nq, :nk], U["bias"], ALU.add)
                    else:
                        P.copy(sb[:nq, :nk], sp_[:nq, :nk], eng=U.get("evac_eng", "act"))

                def s2(u, U):
                    if kindB(u, U):
                        nq, nk = U["nq"], U["nk"]
                        sb = att["sb"][u]
                        P.tt(sb[:nq, :nk], sb[:nq, :nk], U["bias"], ALU.add, eng="pool")

                def s3(u, U):
                    nq, nk = U["nq"], U["nk"]
                    P.rmax(att["nmx"][u][:nq, :], att["sb"][u][:nq, :nk], negate=True)

                def s4(u, U):
                    nq, nk = U["nq"], U["nk"]
                    P.act(att["pbf"][u][:nq, :nk], att["sb"][u][:nq, :nk], AF.Exp, bias=att["nmx"][u][:nq, :],
                          accum=att["rsum"][u][:nq, :])

                def s5(u, U):
                    nq, nk = U["nq"], U["nk"]
                    pt = banks_bf[4 + u % 2]
                    pbf = att["pbf"][u]
                    nkt = (nk + 127) // 128
                    for kt in range(nkt):
                        w = min(128, nk - kt * 128)
                        P.transpose(pt[:w, kt * 128:kt * 128 + nq], pbf[:nq, kt * 128:kt * 128 + w], identb[:nq, :nq])
                    P.recip(att["rinv"][u][:nq, :], att["rsum"][u][:nq, :])

                def s6(u, U):
                    nq, nk = U["nq"], U["nk"]
                    pt = banks_bf[4 + u % 2]
                    ptb = att["ptb"][u]
                    eng = "dve"
                    nfull = nk // 128
                    if nq == 128:
                        ptv = ptb[:, 0:nfull, :]
                        P.copy(V(ptv.t, ptb.base[:, 0:nfull, :].rearrange("p a b -> p (a b)"), ptv.boxes),
                               pt[:, 0:nfull * 128], eng=eng)
                    else:
                        src = pt[:, 0:nfull * 128]
                        P.copy(ptb[:, 0:nfull, :nq],
                               V(src.t, pt.base[:, 0:nfull * 128].rearrange("p (a b) -> p a b", a=nfull)[:, :, :nq], src.boxes),
                               eng=eng)
                    if nk % 128:
                        w = nk % 128
                        P.copy(ptb[:w, nfull, :nq], pt[:w, nfull * 128:nfull * 128 + nq], eng=eng)

                def s7(u, U):
                    nq = U["nq"]
                    ob = banks[4 + u % 2]
                    ptb = att["ptb"][u]
                    groups = U["vparts"]
                    if not isinstance(groups[0], dict):
                        groups = [dict(r0=0, nr=nq, parts=groups)]
                    for g_ in groups:
                        r0, nr, vp = g_["r0"], g_["nr"], g_["parts"]
                        for kt, (vv, nn) in enumerate(vp):
                            P.mm(ob[r0:r0 + nr, 384:448], ptb[:nn, kt, r0:r0 + nr], vv, start=(kt == 0), stop=(kt == len(vp) - 1))

                def s8(u, U):
                    nq = U["nq"]
                    ob = banks[4 + u % 2]
                    P.act(U["out"], ob[:nq, 384:448], AF.Copy, scale=att["rinv"][u][:nq, :])
                    if U.get("post"):
                        U["post"]()

                stages = (s0, s1, s2, s3, s4, s5, s6, s7, s8)
                st["att"] = True
                try:
                    _run(stages, units, n)
                finally:
                    st["att"] = False

            def _run(stages, units, n):
                ns = len(stages)
                if _DBG.get("noskew"):
                    for u in range(n):
                        for si in range(ns):
                            stages[si](u, units[u])
                    return
                for step in range(n + ns - 1):
                    for si in range(ns):
                        u = step - si
                        if 0 <= u < n:
                            stages[si](u, units[u])

            def mem_units(l, mqT, qoff, nq, seg, osb, ocol, post=None):
                us = []
                for h in range(4):
                    c, po = h // 2, 64 * (h % 2)
                    kt = (MKT[:, l, c, :] if prompt else MKs[:, seg, l, c, :])
                    vparts = [((MV[:, l, mt, h * 64:(h + 1) * 64] if prompt else MVs[:, seg, l, mt, h * 64:(h + 1) * 64]), 128)
                              for mt in range(2)]
                    us.append(dict(nq=nq, sparts=[(mqT[:, h, qoff:qoff + nq], kt, NMEM)], bias=None, nk=NMEM,
                                   vparts=vparts, out=osb[:nq, ocol + h * 64:ocol + (h + 1) * 64],
                                   post=(post if h == 3 else None)))
                return us

            def mem_units_pair(l, g, osb, ocol, post=None):
                us = []
                for h in range(4):
                    c = h // 2
                    sg_, vg_ = [], []
                    for e in range(2):
                        b_ = 2 * g + e
                        sg_.append(dict(r0=64 * e, nr=64, parts=[(mqT[:, h, b_ * L:(b_ + 1) * L], MKs[:, b_, l, c, :], NMEM)]))
                        vg_.append(dict(r0=64 * e, nr=64,
                                        parts=[(MVs[:, b_, l, mt, h * 64:(h + 1) * 64], 128) for mt in range(2)]))
                    us.append(dict(nq=128, sparts=sg_, bias=None, nk=NMEM, vparts=vg_,
                                   out=osb[:, ocol + h * 64:ocol + (h + 1) * 64], post=(post if h == 3 else None)))
                return us

            def out_proj_and_ffn(l):
                m1 = A.mark()
                wo = A.alloc([8, D], BF16)
                for n0 in (0, 512):
                    load_w(wo[:, :, n0:n0 + 512], wview(w_out, n0, n0 + 512, lead=l), f"wo{l}_{n0}")
                sst = Stats(xT, NT, sqb, rstd)
                for j in range(8):
                    b = nb()
                    for k in range(8):
                        P.mm(banks[b][:, :NT], wo[:, k, j * 128:(j + 1) * 128], mixT[:, k, :], start=(k == 0), stop=(k == 7))
                    P.tt(xT[:, j, :], xT[:, j, :], banks[b][:, :NT], ALU.add)
                    sst.feed(j)
                sst.finish()
                A.release(m1)
                m1 = A.mark()
                rms_apply(hT, xT, NT, rstd, PP_GFFN + 8 * l, ntmp)
                actT = A.alloc([NJ, NT], BF16)
                m_gu = A.mark()
                wg = [A.alloc([8, 512], BF16) for _ in range(2)]
                wu = [A.alloc([8, 512], BF16) for _ in range(2)]
                sg = [A.alloc([NT], F32) for _ in range(2)]
                for blk in range(6):
                    j0 = blk * 4
                    nj = min(4, NJ - j0)
                    s = blk % 2
                    load_w(wg[s][:, :, :nj * 128], wview(w_gu, j0 * 128, (j0 + nj) * 128, lead=l), f"wg{l}_{blk}")
                    load_w(wu[s][:, :, :nj * 128], wview(w_gu, DFF + j0 * 128, DFF + (j0 + nj) * 128, lead=l), f"wu{l}_{blk}")
                    for jj in range(nj):
                        j = j0 + jj
                        bg = nb()
                        for k in range(8):
                            P.mm(banks[bg][:, :NT], wg[s][:, k, jj * 128:(jj + 1) * 128], hT[:, k, :], start=(k == 0), stop=(k == 7))
                        bu = nb()
                        for k in range(8):
                            P.mm(banks[bu][:, :NT], wu[s][:, k, jj * 128:(jj + 1) * 128], hT[:, k, :], start=(k == 0), stop=(k == 7))
                        P.act(sg[j % 2][:], banks[bg][:, :NT], AF.Silu)
                        P.tt(actT[:, j, :], sg[j % 2][:], banks[bu][:, :NT], ALU.mult)
                A.release(m_gu)
                wd = [A.alloc([NJ, 256], BF16) for _ in range(2)]
                sst2 = Stats(xT, NT, sqb, rstd)
                for blk in range(4):
                    s = blk % 2
                    load_w(wd[s][:], UV(w_dn.ap()[l][:, blk * 256:(blk + 1) * 256].rearrange("(j p) n -> p j n", p=128)), f"wd{l}_{blk}")
                    for nn in range(2):
                        n = blk * 2 + nn
                        b = nb()
                        for j in range(NJ):
                            P.mm(banks[b][:, :NT], wd[s][:, j, nn * 128:(nn + 1) * 128], actT[:, j, :], start=(j == 0), stop=(j == NJ - 1))
                        P.tt(xT[:, n, :], xT[:, n, :], banks[b][:, :NT], ALU.add)
                        sst2.feed(n)
                sst2.finish()
                A.release(m1)

            mqT = A.alloc([4, NT], BF16)

            def evac_q(dst, c2, bank):
                for e in range(2):
                    P.act(dst[64 * e:64 * e + 64, 2 * c2 + e, :], bank[64 * e:64 * e + 64, :NT], AF.Copy, scale=0.125)

            def zero_q(dst, nh):
                for h_ in range(nh):
                    e = h_ % 2
                    P.memset(dst[64 * (1 - e):64 * (1 - e) + 64, h_, :], 0.0, eng="pool")

            def osb_to_mixT(osb, nq, tcol, c_lo, c_hi):
                b = nb()
                for c in range(c_lo, c_hi):
                    P.transpose(banks_bf[b][:, (c - c_lo) * 128:(c - c_lo) * 128 + nq], osb[:nq, c * 128:(c + 1) * 128], identb[:nq, :nq])
                for c in range(c_lo, c_hi):
                    P.copy(mixT[:, c, tcol:tcol + nq], banks_bf[b][:, (c - c_lo) * 128:(c - c_lo) * 128 + nq],
                           eng=("act" if (c_lo // 4) % 2 else "dve"))

            rms_stats(xT, NT, sqb, rstd)
            rms_apply(hT, xT, NT, rstd, PP_GMIX, ntmp)
            m1 = A.mark()
            wA = A.alloc([8, 1792], BF16)
            for q in range(4):
                load_w(wA[:, :, q * 448:(q + 1) * 448], wview(w_in_a, q * 448, (q + 1) * 448), f"wA_{q}")
            class Ring(list):
                def __getitem__(self, c):
                    return list.__getitem__(self, c % len(self))

            xr = Ring(A.alloc([nseg, L + 3], F32) for _ in range(2))
            gg = Ring(A.alloc([NT], F32) for _ in range(5))
            xc = Ring(A.alloc([nseg, L], F32) for _ in range(4))
            xcb = Ring(A.alloc([nseg, L], BF16) for _ in range(2))
            ra = Ring(A.alloc([nseg, L], F32) for _ in range(2))
            iu = Ring(A.alloc([nseg, L], F32) for _ in range(2))
            a2 = Ring(A.alloc([nseg, L], F32) for _ in range(2))
            hs = Ring(A.alloc([nseg, L], F32) for _ in range(2))
            cout = A.alloc([6, nseg, 3], F32)
            lout = A.alloc([6, nseg], F32)
            lbank = {}

            def as2d(v, buf):
                return V(v.t, buf.base.rearrange("p s l -> p (s l)"), v.boxes)

            def L1(c):
                s = c
                b = nb()
                for k in range(8):
                    P.mm(banks[b][:, :NT], wA[:, k, c * 128:(c + 1) * 128], hT[:, k, :], start=(k == 0), stop=(k == 7))
                b2 = nb()
                for k in range(8):
                    P.mm(banks[b2][:, :NT], wA[:, k, 768 + c * 128:768 + (c + 1) * 128], hT[:, k, :], start=(k == 0), stop=(k == 7))
                lbank[c] = (b, b2)

            def L2(c):
                s = c
                b, b2 = lbank[c]
                if prompt:
                    P.copy(xr[s][:, 0, 0:3], convst[:, c, :], eng="pool")
                    P.copy(xr[s][:, 0, 3:3 + L], banks[b][:, :NT], eng="act")
                else:
                    P.copy(xr[s][:, :, 0:3], cst[:, c, :, :], eng="pool")
                    P.copy(xr[s][:, :, 3:3 + L],
                           V(PS.t, banks[b].base[:, :NT].rearrange("p (s l) -> p s l", s=nseg), banks[b][:, :NT].boxes), eng="act")
                P.act(gg[s][:], banks[b2][:, :NT], AF.Gelu_apprx_tanh)

            def L3(c):
                s = c
                cw = PP_CONVW + c * 4
                P.ts(xc[s][:], xr[s][:, :, 3:3 + L], ppc(cw + 3), ppc(PP_CONVB + c), ALU.mult, ALU.add)
                for kk in range(3):
                    if kk == 2:
                        P.stt(xcb[s][:], xr[s][:, :, kk:kk + L], ppc(cw + kk), xc[s][:], ALU.mult, ALU.add)
                    P.stt(xc[s][:], xr[s][:, :, kk:kk + L], ppc(cw + kk), xc[s][:], ALU.mult, ALU.add)
                if prompt:
                    P.copy(convst[:, c, :], xr[s][:, 0, L:L + 3], eng="pool")
                else:
                    P.copy(cout[:, c, :, :], xr[s][:, :, L:L + 3], eng="pool")

            def L4(c):
                s = c
                xcb2 = as2d(xcb[s][:], xcb[s])
                br = nb()
                P.mm(banks[br][:, :NT], wa_bd[:, c, :], xcb2)
                bi_ = nb()
                P.mm(banks[bi_][:, :NT], wi_bd[:, c, :], xcb2)
                lbank[(c, "g")] = (br, bi_)

            def L5(c):
                s = c
                br, bi_ = lbank[(c, "g")]
                ra2 = as2d(ra[s][:], ra[s])
                iu2 = as2d(iu[s][:], iu[s])
                a22 = as2d(a2[s][:], a2[s])
                P.act(ra2, banks[br][:, :NT], AF.Tanh, bias=hba[:, c:c + 1], scale=0.5)
                P.act(iu2, banks[bi_][:, :NT], AF.Tanh, bias=hbi[:, c:c + 1], scale=0.5)
                P.act(ra2, ra2, AF.Exp, scale=cfeat2[:, c:c + 1], bias=cfeat2[:, c:c + 1])

            def L6(c):
                s = c
                a22 = as2d(a2[s][:], a2[s])
                ra2 = as2d(ra[s][:], ra[s])
                P.ts(a22, ra2, -1.0, 1.0, ALU.mult, ALU.add)
                P.stt(a22, ra2, 1.0, a22, ALU.add, ALU.mult)
                P.act(a22, a22, AF.Sqrt)
                if prompt and pi == 0:
                    P.memset(a2[s][:, 0, 0:1], 1.0, eng="dve")

            def L7(c):
                s = c
                xc2 = as2d(xc[s][:], xc[s])
                iu2 = as2d(iu[s][:], iu[s])
                a22 = as2d(a2[s][:], a2[s])
                P.stt(iu2, iu2, 1.0, xc2, ALU.add, ALU.mult)
                P.stt(iu2, iu2, 0.5, a22, ALU.mult, ALU.mult)
                for sg_ in range(nseg):
                    init = lrust[:, c:c + 1] if prompt else lst0[:, c, sg_:sg_ + 1]
                    P.scan(hs[s][:, sg_, :], ra[s][:, sg_, :], iu[s][:, sg_, :], init)
                if prompt:
                    P.copy(lrust[:, c:c + 1], hs[s][:, 0, L - 1:L], eng="dve")
                else:
                    P.copy(lout[:, c, :], hs[s][:, :, L - 1], eng="dve")
                P.tt(mixT[:, c, :], as2d(hs[s][:], hs[s]), gg[s][:], ALU.mult)

            def L56(c):
                L5(c)
                L6(c)

            lst_ = (L1, L2, L3, L4, L56, L7)
            for step in range(6 + len(lst_) - 1):
                for si, fn in enumerate(lst_):
                    c = step - si
                    if 0 <= c < 6:
                        fn(c)
            for c in range(2):
                b = nb()
                for k in range(8):
                    P.mm(banks[b][:, :NT], wA[:, k, 1536 + c * 128:1536 + (c + 1) * 128], hT[:, k, :], start=(k == 0), stop=(k == 7))
                if c == 0:
                    zero_q(mqT, 4)
                evac_q(mqT, c, banks[b])
            if prompt and pi == 3:
                for c in range(6):
                    P.dma("sp", UV(conv_p.ap()[:, c * 128:(c + 1) * 128].rearrange("t p -> p t")), convst[:, c, :], **NCD)
                    P.dma("sp", UV(lru_p.ap()[c * 128:(c + 1) * 128].rearrange("(p o) -> p o", o=1)), lrust[:, c:c + 1], **NCD)
            if not prompt:
                for c in range(6):
                    P.dma("sp", UV(conv_s.ap()[:, :, c * 128:(c + 1) * 128].rearrange("b t p -> p b t")), cout[:, c, :, :], **NCD)
                    P.dma("sp", UV(lru_s.ap()[:, c * 128:(c + 1) * 128].rearrange("b p -> p b")), lout[:, c, :], **NCD)
            A.release(m1)
            _stage(f"{kind}{pi}_lru")
            if prompt and pi == 0:
                preconvert(2)
            m1 = A.mark()
            att = alloc_att()
            osbs = [A.alloc([D], BF16) for _ in range(2)]
            units = []
            ngrp = ntile
            gq = 128
            for g in range(ngrp):
                ob_ = osbs[g % 2]
                post_ = (lambda ob_=ob_, g=g: osb_to_mixT(ob_, gq, g * gq, 6, 8))
                if prompt:
                    units += mem_units(0, mqT, g * gq, gq, g, ob_, 768, post=post_)
                else:
                    units += mem_units_pair(0, g, ob_, 768, post=post_)
            for U_ in units:
                U_["evac_eng"] = "dve"
            run_units(units, att)
            A.release(m1)
            _stage(f"{kind}{pi}_mem0")
            if prompt and pi == 0:
                preconvert(3)
            out_proj_and_ffn(0)
            _stage(f"{kind}{pi}_ffn0")

            hT2 = A.alloc([8, NT], BF16)
            if not prompt:
                KTs = A.alloc([6, NT], BF16)
                VS = A.alloc([NSB, ATT], BF16)
            m1 = A.mark()
            rms_apply(hT, xT, NT, rstd, PP_GKV, ntmp)
            rms_apply(hT2, xT, NT, rstd, PP_GMIX + 8, ntmp)
            wkv = A.alloc([8, 1536], BF16)
            for q in range(3):
                load_w(wkv[:, :, q * 512:(q + 1) * 512], wview(w_kv, q * 512, (q + 1) * 512), f"wkv_{q}")
            if prompt:
                KTd, kcol = KT, t0
            else:
                KTd, kcol = KTs, 0
            for c in range(6):
                b = nb()
                for k in range(8):
                    P.mm(banks[b][:, :NT], wkv[:, k, c * 128:(c + 1) * 128], hT[:, k, :], start=(k == 0), stop=(k == 7))
                P.copy(KTd[:, c, kcol:kcol + NT], banks[b][:, :NT], eng="act")
            kvst = [A.alloc([1536], F32) for _ in range(2)]
            ntok_tiles = ntile if prompt else nseg
            tl = 128 if prompt else L
            for t in range(ntok_tiles):
                need_out = (not prompt) or (pi == 3)
                s = t % 2
                for (n0, nw) in ((0, 512), (512, 512), (1024, 512)):
                    if n0 < 768 and not need_out:
                        if n0 == 0:
                            continue
                    b = nb()
                    for k in range(8):
                        P.mm(banks[b][:tl, :nw], hT[:, k, t * tl:(t + 1) * tl], wkv[:, k, n0:n0 + nw], start=(k == 0), stop=(k == 7))
                    if need_out:
                        P.copy(kvst[s][:tl, n0:n0 + nw], banks[b][:tl, :nw], eng="act")
                    v0 = max(n0, 768)
                    if v0 < n0 + nw:
                        src = banks[b][:tl, v0 - n0:nw]
                        if prompt:
                            P.copy(VP[:, pi * 4 + t, v0 - 768:n0 + nw - 768], src, eng="dve")
                        else:
                            P.copy(VS[:tl, t, v0 - 768:n0 + nw - 768], src, eng="dve")
                if need_out:
                    if prompt:
                        P.dma("sp", k_p[t * 128:(t + 1) * 128, :], kvst[s][:, 0:768])
                        P.dma("sp", v_p[t * 128:(t + 1) * 128, :], kvst[s][:, 768:1536])
                    else:
                        P.dma("sp", k_s[t * L:(t + 1) * L, :], kvst[s][:L, 0:768])
                        P.dma("sp", v_s[t * L:(t + 1) * L, :], kvst[s][:L, 768:1536])
            A.release(m1)

            _stage(f"{kind}{pi}_kv")
            m1 = A.mark()
            qT = A.alloc([12, NT], BF16)
            zero_q(qT, 12)
            m2 = A.mark()
            wB = A.alloc([8, D], BF16)
            for n0 in (0, 512):
                load_w(wB[:, :, n0:n0 + 512], wview(w_in_b, n0, n0 + 512), f"wB_{n0}")
            for c in range(8):
                b = nb()
                for k in range(8):
                    P.mm(banks[b][:, :NT], wB[:, k, c * 128:(c + 1) * 128], hT2[:, k, :], start=(k == 0), stop=(k == 7))
                if c < 6:
                    evac_q(qT, c, banks[b])
                else:
                    evac_q(mqT, c - 6, banks[b])
            A.release(m2)
            att = alloc_att()
            osbs = [A.alloc([D], BF16) for _ in range(2)]
            units = []
            if prompt:
                for t in range(ntile):
                    J = pi * 4 + t
                    k0 = max(0, 128 * J - 512)
                    k1 = 128 * J + 128
                    nk = k1 - k0
                    ob_ = osbs[t % 2]
                    for h in range(12):
                        c, po = h // 2, 64 * (h % 2)
                        vparts = [(VP[:, k0 // 128 + kt, h * 64:(h + 1) * 64], 128) for kt in range(nk // 128)]
                        units.append(dict(nq=128, sparts=[(qT[:, h, t * 128:(t + 1) * 128], KT[:, c, k0:k1], nk)],
                                          bias=biasb[:, h, 640 - nk:640], nk=nk, vparts=vparts,
                                          out=ob_[:, h * 64:(h + 1) * 64], post=None))

                    def post(ob_=ob_, t=t):
                        osb_to_mixT(ob_, 128, t * 128, 0, 4)
                        osb_to_mixT(ob_, 128, t * 128, 4, 8)
                    units += mem_units(1, mqT, t * 128, 128, 0, ob_, 768, post=post)
                run_units(units, att)
            else:
                kctok = A.alloc([4, ATT], BF16)
                KcT = [A.alloc([6, 512], BF16) for _ in range(2)]
                Vc = [A.alloc([4, ATT], BF16) for _ in range(2)]
                for h0 in range(0, 12, 4):
                    P.dma("sp", biasb[:, h0:h0 + 4, :], bias_s_d[:, h0:h0 + 4, :])

                def load_k(g):
                    for e in range(2):
                        b_ = 2 * g + e
                        P.dma("pool", kctok[:], UV(c_k.ap()[b_].rearrange("(t p) d -> p t d", p=128)))
                        for c in range(6):
                            b = nb()
                            for kt in range(4):
                                P.transpose(banks_bf[b][:, kt * 128:(kt + 1) * 128], kctok[:, kt, c * 128:(c + 1) * 128], identb[:])
                            P.copy(KcT[e][:, c, :], banks_bf[b][:, 0:512], eng=("act" if c % 2 else "dve"))

                def load_v(g):
                    for e in range(2):
                        P.dma("pool", Vc[e][:], UV(c_v.ap()[2 * g + e].rearrange("(t p) d -> p t d", p=128)))

                for g in range(nseg // 2):
                    ob_ = osbs[g % 2]
                    for h in range(12):
                        c = h // 2
                        sg_, vg_ = [], []
                        for e in range(2):
                            b_ = 2 * g + e
                            qv = qT[:, h, b_ * L:(b_ + 1) * L]
                            sg_.append(dict(r0=64 * e, nr=64,
                                            parts=[(qv, KcT[e][:, c, :], 512), (qv, KTs[:, c, b_ * L:(b_ + 1) * L], L)]))
                            vp_ = [(Vc[e][:, kt, h * 64:(h + 1) * 64], 128) for kt in range(4)]
                            vp_.append((VS[:L, b_, h * 64:(h + 1) * 64], L))
                            vg_.append(dict(r0=64 * e, nr=64, parts=vp_))
                        units.append(dict(nq=128, sparts=sg_, bias=biasb[:, h, 0:576], nk=576, vparts=vg_,
                                          out=ob_[:, h * 64:(h + 1) * 64], post=None))

                    def post(ob_=ob_, g=g):
                        osb_to_mixT(ob_, 128, g * 128, 0, 4)
                        osb_to_mixT(ob_, 128, g * 128, 4, 8)
                    units += mem_units_pair(1, g, ob_, 768, post=post)
                units[0]["pre"] = lambda: (load_k(0), load_v(0))
                units[16]["pre"] = lambda: load_k(1)
                units[16 + 7]["pre"] = lambda: load_v(1)
                run_units(units, att)
            A.release(m1)
            _stage(f"{kind}{pi}_attn")
            out_proj_and_ffn(1)
            _stage(f"{kind}{pi}_ffn1")

            m1 = A.mark()
            yT = A.alloc([8, NT], F32)
            for k in range(8):
                P.stt(yT[:, k, :], xT[:, k, :], ppc(PP_GFIN + k), rstd[:, :NT], ALU.mult, ALU.mult)
                P.dma("sp", UV(ydst.ap()[k * 128:(k + 1) * 128, t0:t0 + NT]), yT[:, k, :])
            A.release(m1)
            A.release(mp)
            _stage(f"{kind}{pi}_end")

        _stage("prologue")
        for pi in range(4):
            run_pass("p", pi)
        run_pass("s", 0)

    try:
        body()
    except _Stop:
        pass
    _DBG["P"] = P
    P.build()
    print("arena peak (bf16 elems):", A.peak, "ops:", {e: len(P.ops[e]) for e in ENGS})
    return nc


_NC_CACHE = {}


def _host_layout(inputs):
    f = lambda a: np.ascontiguousarray(np.asarray(a, dtype=np.float32))
    pp = np.zeros((128, PP_N), np.float32)

    def colmajor(v):
        return f(v).reshape(-1, 128).T

    for l in range(2):
        pp[:, PP_GMIX + 8 * l:PP_GMIX + 8 * l + 8] = colmajor(inputs["g_mix"][l])
        pp[:, PP_GFFN + 8 * l:PP_GFFN + 8 * l + 8] = colmajor(inputs["g_ffn"][l])
        pp[:, PP_GMEM + 8 * l:PP_GMEM + 8 * l + 8] = colmajor(inputs["g_mem"][l])
    pp[:, PP_GFIN:PP_GFIN + 8] = colmajor(inputs["g_final"])
    pp[:, PP_GKV:PP_GKV + 8] = colmajor(inputs["g_kv"])
    cw = f(inputs["conv_w"])[0]
    for c in range(6):
        pp[:, PP_CONVW + c * 4:PP_CONVW + c * 4 + 4] = cw[:, c * 128:(c + 1) * 128].T
    pp[:, PP_CONVB:PP_CONVB + 6] = colmajor(f(inputs["conv_b"])[0])
    pp[:, PP_BA:PP_BA + 6] = colmajor(f(inputs["b_rg_a"])[0])
    pp[:, PP_BI:PP_BI + 6] = colmajor(f(inputs["b_rg_i"])[0])
    pp[:, PP_LAM:PP_LAM + 6] = colmajor(f(inputs["lru_lambda"])[0])

    def blockdiag(w):
        w = f(w)[0]
        o = np.zeros((128, 6, 128), np.float32)
        for n in range(12):
            c, r = n // 2, 64 * (n % 2)
            o[r:r + 64, c, r:r + 64] = w[n]
        return o

    rel = f(inputs["rel_bias"])[0]
    qi = np.arange(64)[:, None]
    kj = np.arange(576)[None, :]
    idx = np.clip(512 + qi - kj, -256, 256) + 256
    b64 = rel[idx]
    b64 = np.transpose(b64, (0, 2, 1))
    bias_full = np.full((128, 12, 640), NEG, np.float32)
    bias_full[0:64, :, 0:576] = b64
    bias_full[64:128, :, 64:640] = b64
    bias_s = np.full((128, 12, 640), NEG, np.float32)
    bias_s[0:64, :, 0:576] = b64
    bias_s[64:128, :, 0:576] = b64
    return pp, blockdiag(inputs["w_rg_a"]), blockdiag(inputs["w_rg_i"]), bias_full, bias_s


def kernel(**inputs):
    f = lambda a: np.ascontiguousarray(np.asarray(a, dtype=np.float32))
    if "nc" not in _NC_CACHE:
        _NC_CACHE["nc"] = build_program()
    nc = _NC_CACHE["nc"]
    pp, wa_bd, wi_bd, bias_full, bias_s = _host_layout(inputs)
    ident = np.eye(128, dtype=np.float32)
    shared = {
        "pp": pp, "ident": ident, "bias_full": bias_full, "bias_s": bias_s, "wa_bd": wa_bd, "wi_bd": wi_bd,
        "w_in_a": f(inputs["w_in_a"])[0], "w_kv": f(inputs["w_kv"]), "w_in_b": f(inputs["w_in_b"])[0],
        "w_mem_kv": f(inputs["w_mem_kv"]), "w_out": f(inputs["w_out"]),
        "w_gu": f(inputs["w_ffn_gu"]), "w_dn": f(inputs["w_ffn_down"]),
    }
    xp = f(inputs["x_prompt"])
    xs = f(inputs["x_sample"])
    sc = f(inputs["state_conv"])[0]
    sl = f(inputs["state_lru"])[0]
    ck = f(inputs["cache_k"]).reshape(32, 512, ATT)
    cv = f(inputs["cache_v"]).reshape(32, 512, ATT)
    cmk = f(inputs["cache_mem_k"]).reshape(2, 32, NMEM, MEMW)
    cmv = f(inputs["cache_mem_v"]).reshape(2, 32, NMEM, MEMW)
    mp = f(inputs["mem_prompt"])
    in_maps = []
    for c in range(8):
        sb = slice(4 * c, 4 * c + 4)
        m = dict(shared)
        m.update({
            "x_p": np.ascontiguousarray(xp[c].T), "x_s": np.ascontiguousarray(xs[sb].reshape(NSB * LS, D).T),
            "st_conv": np.ascontiguousarray(sc[sb]), "st_lru": np.ascontiguousarray(sl[sb]),
            "c_k": np.ascontiguousarray(ck[sb]), "c_v": np.ascontiguousarray(cv[sb]),
            "c_mk": np.ascontiguousarray(cmk[:, sb]), "c_mv": np.ascontiguousarray(cmv[:, sb]),
            "mem_p": np.ascontiguousarray(mp[c].T),
        })
        in_maps.append(m)
    res = run_bass_kernel_spmd(nc, in_maps, core_ids=list(range(8)))
    R = res.results
    st = lambda name: np.stack([np.asarray(R[c][name], dtype=np.float32) for c in range(8)])
    y_prompt = np.ascontiguousarray(np.transpose(st("y_p"), (0, 2, 1)))
    y_sample = np.ascontiguousarray(np.transpose(st("y_s"), (0, 2, 1))).reshape(32, LS, D)
    conv_pp = st("conv_p")[None]
    lru_pp = st("lru_p")[None]
    k_pp = st("k_p").reshape(8, 512, 12, 64)
    v_pp = st("v_p").reshape(8, 512, 12, 64)
    mk = np.transpose(st("mk_p"), (1, 0, 2, 3)).reshape(2, 8, NMEM, 4, 64)
    mv = np.transpose(st("mv_p"), (1, 0, 2, 3)).reshape(2, 8, NMEM, 4, 64)
    conv_ss = st("conv_s").reshape(32, 3, LRU)[None]
    lru_ss = st("lru_s").reshape(32, LRU)[None]
    k_ss = st("k_s").reshape(32, LS, 12, 64)
    v_ss = st("v_s").reshape(32, LS, 12, 64)
    return (y_prompt, y_sample, conv_pp, lru_pp, k_pp, v_pp,
            np.ascontiguousarray(mk), np.ascontiguousarray(mv), conv_ss, lru_ss, k_ss, v_ss)
```
